# Optimizing a Trainium2 kernel written in Bass

```python
import math
import jax, jax.numpy as jnp
from jax import lax
import numpy as np

D_MODEL = 1024
BATCH = 8
SEQ = 8192
DEPTH = 1

N_META = 16
BLOCK = 128
PAD = BLOCK - N_META

HG_HEADS = 4
HG_K = 128
HG_V = 128
HG_KW = HG_HEADS * HG_K
HG_VW = HG_HEADS * HG_V
SUB = 16

ATT_HEADS = 8
ATT_KV_HEADS = 2
HEAD_DIM = 64
ATT_QW = ATT_HEADS * HEAD_DIM
ATT_KVW = ATT_KV_HEADS * HEAD_DIM
WINDOW = 128
ROPE_THETA = 10000.0

N_BRANCH = 2
D_FF = ((-(-8 * D_MODEL // 3) + 255) // 256) * 256
EPS = 1e-5
ALPHA = (2.0 * DEPTH) ** 0.25
BETA = (8.0 * DEPTH) ** -0.25
SPLIT_SIZES = (HG_KW, HG_KW, HG_VW, HG_VW, ATT_QW, ATT_KVW, ATT_KVW, N_BRANCH * D_MODEL)
IN_W = sum(SPLIT_SIZES)

kernel_name = "hgrn2_swa_sink_hybrid_deepnorm"


def layer_norm(x, g, b):
    xf = x.astype(jnp.float32)
    mu = jnp.mean(xf, axis=-1, keepdims=True)
    var = jnp.mean(jnp.square(xf - mu), axis=-1, keepdims=True)
    y = (xf - mu) * lax.rsqrt(var + EPS) * g.astype(jnp.float32) + b.astype(jnp.float32)
    return y.astype(x.dtype)


def rope(x, pos):
    half = HEAD_DIM // 2
    inv = ROPE_THETA ** (-jnp.arange(half, dtype=jnp.float32) / half)
    ang = pos.astype(jnp.float32)[:, None] * inv[None, :]
    cos = jnp.cos(ang)[None, :, None, :]
    sin = jnp.sin(ang)[None, :, None, :]
    xf = x.astype(jnp.float32)
    x1, x2 = xf[..., :half], xf[..., half:]
    return jnp.concatenate([x1 * cos - x2 * sin, x2 * cos + x1 * sin], axis=-1).astype(x.dtype)


def hgrn2_chunk(state, inp):
    q, k, v, log_f = inp
    B, H, C, K = q.shape
    V = v.shape[-1]
    n = C // SUB
    b = jnp.cumsum(log_f, axis=2)
    b_last = b[:, :, -1]
    o_inter = jnp.einsum('bhck,bhkv->bhcv', q * jnp.exp(b), state)
    qs = q.reshape(B, H, n, SUB, K)
    ks = k.reshape(B, H, n, SUB, K)
    vs = v.reshape(B, H, n, SUB, V)
    bs = b.reshape(B, H, n, SUB, K)
    tri = jnp.tril(jnp.ones((SUB, SUB), dtype=bool))[:, :, None]
    diff = bs[:, :, :, :, None, :] - bs[:, :, :, None, :, :]
    decay = jnp.exp(jnp.where(tri, diff, -jnp.inf))
    a_diag = jnp.einsum('bhntk,bhnsk,bhntsk->bhnts', qs, ks, decay)
    o_diag = jnp.einsum('bhnts,bhnsv->bhntv', a_diag, vs)
    b_ref = jnp.concatenate([jnp.zeros_like(bs[:, :, :1, 0]), bs[:, :, :-1, -1]], axis=2)
    q_off = qs * jnp.exp(bs - b_ref[:, :, :, None])
    earlier = (jnp.arange(C) // SUB)[None, :] < jnp.arange(n)[:, None]
    k_off = k[:, :, None] * jnp.exp(jnp.where(earlier[:, :, None],
                                              b_ref[:, :, :, None] - b[:, :, None], -jnp.inf))
    a_off = jnp.einsum('bhntk,bhnsk->bhnts', q_off, k_off)
    o_off = jnp.einsum('bhnts,bhsv->bhntv', a_off, v)
    o = o_inter + (o_diag + o_off).reshape(B, H, C, V)
    new_state = state * jnp.exp(b_last)[..., None] + jnp.einsum(
        'bhck,bhcv->bhkv', k * jnp.exp(b_last[:, :, None] - b), v)
    return new_state, o


def hgrn2_mixer(q_raw, f_raw, i_raw, g_raw, lower_bound, norm_g, valid):
    B, P, _ = q_raw.shape
    N = P // BLOCK
    f32 = jnp.float32
    q = jax.nn.silu(q_raw.astype(f32))
    fg = lower_bound + (1.0 - lower_bound) * jax.nn.sigmoid(f_raw.astype(f32))
    m = valid[None, :, None]
    log_f = jnp.where(m, jnp.log(fg), 0.0)
    k = jnp.where(m, 1.0 - fg, 0.0)
    v = i_raw.astype(f32)

    def to_chunks(t, dh):
        return t.reshape(B, N, BLOCK, HG_HEADS, dh).transpose(1, 0, 3, 2, 4)

    xs = (to_chunks(q, HG_K), to_chunks(k, HG_K), to_chunks(v, HG_V), to_chunks(log_f, HG_K))
    s0 = jnp.zeros((B, HG_HEADS, HG_K, HG_V), f32)
    _, o = lax.scan(hgrn2_chunk, s0, xs)
    o = o.transpose(1, 0, 3, 2, 4).reshape(B, P, HG_HEADS, HG_V)
    o = o * lax.rsqrt(jnp.mean(o * o, axis=-1, keepdims=True) + EPS) * norm_g.astype(f32)
    gate = jax.nn.silu(g_raw.astype(f32)).reshape(B, P, HG_HEADS, HG_V)
    return (o * gate).reshape(B, P, HG_VW).astype(q_raw.dtype)


def swa_sink_attention(q_raw, k_raw, v_raw, sinks, pos):
    B, P, _ = q_raw.shape
    NB = P // BLOCK
    G = ATT_HEADS // ATT_KV_HEADS
    f32 = jnp.float32
    q = rope(q_raw.reshape(B, P, ATT_HEADS, HEAD_DIM), pos).astype(f32)
    k = rope(k_raw.reshape(B, P, ATT_KV_HEADS, HEAD_DIM), pos).astype(f32)
    v = v_raw.reshape(B, P, ATT_KV_HEADS, HEAD_DIM).astype(f32)
    scale = HEAD_DIM ** -0.5
    qb = q.reshape(B, NB, BLOCK, ATT_KV_HEADS, G, HEAD_DIM)
    kb = k.reshape(B, NB, BLOCK, ATT_KV_HEADS, HEAD_DIM)
    vb = v.reshape(B, NB, BLOCK, ATT_KV_HEADS, HEAD_DIM)
    shift = lambda t: jnp.pad(t, ((0, 0), (1, 0), (0, 0), (0, 0), (0, 0)))[:, :-1]
    k_band = jnp.concatenate([shift(kb), kb], axis=2)
    v_band = jnp.concatenate([shift(vb), vb], axis=2)
    k_meta = k[:, PAD:BLOCK]
    v_meta = v[:, PAD:BLOCK]
    pos_b = pos.reshape(NB, BLOCK)
    pos_prev = jnp.concatenate([jnp.full((1, BLOCK), -1, pos.dtype), pos_b[:-1]], axis=0)
    key_pos = jnp.concatenate([pos_prev, pos_b], axis=1)[:, None, :]
    meta_pos = pos[PAD:BLOCK][None, None, :]
    qp = pos_b[:, :, None]
    band_ok = (key_pos >= N_META) & (key_pos <= qp) & (qp - key_pos < WINDOW)
    meta_ok = meta_pos <= qp
    neg = jnp.finfo(f32).min
    s_band = jnp.einsum('bnqhgd,bnkhd->bnhgqk', qb, k_band) * scale
    s_meta = jnp.einsum('bnqhgd,bmhd->bnhgqm', qb, k_meta) * scale
    s_band = jnp.where(band_ok[None, :, None, None], s_band, neg)
    s_meta = jnp.where(meta_ok[None, :, None, None], s_meta, neg)
    sink = jnp.broadcast_to(sinks.astype(f32).reshape(ATT_KV_HEADS, G)[None, None, :, :, None, None],
                            s_meta.shape[:-1] + (1,))
    p = jax.nn.softmax(jnp.concatenate([s_meta, s_band, sink], axis=-1), axis=-1)
    p_meta = p[..., :N_META]
    p_band = p[..., N_META:N_META + 2 * BLOCK]
    o = (jnp.einsum('bnhgqm,bmhd->bnqhgd', p_meta, v_meta)
         + jnp.einsum('bnhgqk,bnkhd->bnqhgd', p_band, v_band))
    return o.reshape(B, P, ATT_QW).astype(q_raw.dtype)


def setup_inputs(seed: int = 0) -> dict:
    key = jax.random.key(seed)
    ks = jax.random.split(key, 18)
    f32 = jnp.float32
    nrm = lambda k, shape, s: jax.random.normal(k, shape, f32) * s
    return {
        "x": nrm(ks[0], (BATCH, SEQ, D_MODEL), 1.0),
        "meta_tokens": nrm(ks[1], (N_META, D_MODEL), 1.0),
        "ln_emb_g": 1.0 + nrm(ks[2], (D_MODEL,), 0.02),
        "ln_emb_b": nrm(ks[3], (D_MODEL,), 0.02),
        "w_in": nrm(ks[4], (DEPTH, D_MODEL, IN_W), D_MODEL ** -0.5),
        "hg_lower_bounds": nrm(ks[5], (DEPTH + 1, HG_KW), 0.1),
        "hg_norm_g": 1.0 + nrm(ks[6], (DEPTH, HG_V), 0.02),
        "attn_sinks": nrm(ks[7], (DEPTH, ATT_HEADS), 0.5),
        "w_branch_hg": nrm(ks[8], (DEPTH, HG_VW, D_MODEL), HG_VW ** -0.5),
        "w_branch_attn": nrm(ks[9], (DEPTH, ATT_QW, D_MODEL), ATT_QW ** -0.5),
        "w_out": nrm(ks[10], (DEPTH, D_MODEL, D_MODEL), BETA * D_MODEL ** -0.5),
        "ln1_g": 1.0 + nrm(ks[11], (DEPTH, D_MODEL), 0.02),
        "ln1_b": nrm(ks[12], (DEPTH, D_MODEL), 0.02),
        "w_ffn_in": nrm(ks[13], (DEPTH, D_MODEL, 2 * D_FF), D_MODEL ** -0.5),
        "w_ffn_out": nrm(ks[14], (DEPTH, D_FF, D_MODEL), BETA * D_FF ** -0.5),
        "ln2_g": 1.0 + nrm(ks[15], (DEPTH, D_MODEL), 0.02),
        "ln2_b": nrm(ks[16], (DEPTH, D_MODEL), 0.02),
    }


def reference(x, meta_tokens, ln_emb_g, ln_emb_b, w_in, hg_lower_bounds, hg_norm_g, attn_sinks,
              w_branch_hg, w_branch_attn, w_out, ln1_g, ln1_b, w_ffn_in, w_ffn_out, ln2_g, ln2_b):
    B, S, D = x.shape
    P = S + BLOCK
    meta = jnp.broadcast_to(meta_tokens.astype(x.dtype)[None], (B, N_META, D))
    h = layer_norm(jnp.concatenate([meta, x], axis=1), ln_emb_g, ln_emb_b)
    h = jnp.pad(h, ((0, 0), (PAD, 0), (0, 0)))
    pos = jnp.arange(P, dtype=jnp.int32) - PAD
    valid = pos >= 0
    lbs = jnp.cumsum(jax.nn.softmax(hg_lower_bounds.astype(jnp.float32), axis=0), axis=0)
    split_idx = [sum(SPLIT_SIZES[:i + 1]) for i in range(len(SPLIT_SIZES) - 1)]
    for l in range(DEPTH):
        proj = h @ w_in[l]
        hq, hf, hi, hg, aq, ak, av, gates = jnp.split(proj, split_idx, axis=-1)
        y_hg = hgrn2_mixer(hq, hf, hi, hg, lbs[l], hg_norm_g[l], valid) @ w_branch_hg[l]
        y_att = swa_sink_attention(aq, ak, av, attn_sinks[l], pos) @ w_branch_attn[l]
        g_hg, g_att = jnp.split(jax.nn.sigmoid(gates), N_BRANCH, axis=-1)
        mix = (g_hg * y_hg + g_att * y_att) @ w_out[l]
        h = layer_norm(ALPHA * h + mix, ln1_g[l], ln1_b[l])
        a, u = jnp.split(h @ w_ffn_in[l], 2, axis=-1)
        h = layer_norm(ALPHA * h + (jax.nn.silu(a) * u) @ w_ffn_out[l], ln2_g[l], ln2_b[l])
    return h[:, BLOCK:]
```

```python
import math
from contextlib import ExitStack

import numpy as np
import concourse.bass as bass
import concourse.mybir as mybir
from concourse.bass_utils import run_bass_kernel_spmd

F32 = mybir.dt.float32
BF16 = mybir.dt.bfloat16
AF = mybir.ActivationFunctionType
ALU = mybir.AluOpType

D = 1024
SEQ = 8192
NTILES = SEQ // 128 + 1
N_META = 16
PAD = 112
IN_W = 4864
D_FF = 2816
EPS = 1e-5
ALPHA = 2.0 ** 0.25
NCORES = 8
DMA_SCRATCH = 4096
PREFETCH = True
PIPELINE = True
VAR = set()


class Buf:
    __slots__ = ("name", "w", "r")

    def __init__(self, name):
        self.name = name
        self.w = None
        self.r = []


class Op:
    __slots__ = ("eng", "fn", "deps", "idx", "milestone", "semval", "dma", "dmacount", "reads", "writes")

    def __init__(self, eng, fn, dma, reads, writes):
        self.eng = eng
        self.fn = fn
        self.deps = {}
        self.milestone = False
        self.semval = None
        self.dma = dma
        self.dmacount = None
        self.reads = list(reads)
        self.writes = list(writes)


class Sched:
    ENGS = ("pe", "act", "dve", "pool", "sp")

    def __init__(self):
        self.ops = []
        self.cur = None
        self.dma_counts = {}

    def op(self, eng, fn, reads=(), writes=(), dma=None):
        o = Op(eng, fn, dma, reads, writes)
        (self.cur if self.cur is not None else self.ops).append(o)
        return o

    def resolve(self):
        self.dma_counts = {}
        for o in self.ops:
            reads, writes = o.reads, o.writes
            wset = set(id(b) for b in writes)
            for b in reads:
                if b.w is not None:
                    o.deps[id(b.w)] = (b.w, "RAW")
            for b in writes:
                if b.w is not None and id(b.w) not in o.deps:
                    o.deps[id(b.w)] = (b.w, "WAW")
                for r in b.r:
                    if id(r) not in o.deps:
                        o.deps[id(r)] = (r, "WAR")
            for b in writes:
                b.w = o
                b.r = []
            for b in reads:
                if id(b) not in wset:
                    b.r.append(o)
            if o.dma is not None:
                self.dma_counts[o.dma] = self.dma_counts.get(o.dma, 0) + 1
                o.dmacount = self.dma_counts[o.dma]

    @staticmethod
    def _needed(o, d, kind):
        if d.dma is not None:
            return True
        if d.eng == o.eng and kind != "RAW":
            return False
        return True

    def finalize(self):
        self.resolve()
        for o in self.ops:
            for d, kind in o.deps.values():
                if d.dma is None and self._needed(o, d, kind):
                    d.milestone = True
        last = {}
        for o in self.ops:
            if o.dma is None:
                last[o.eng] = o
        for o in last.values():
            o.milestone = True
        cnt = {e: 0 for e in self.ENGS}
        for o in self.ops:
            if o.dma is None and o.milestone:
                cnt[o.eng] += 1
                o.semval = cnt[o.eng]

    def emit(self, nc, block, sems, dma_sems):
        engobj = {"pe": "tensor", "act": "scalar", "dve": "vector", "pool": "gpsimd", "sp": "sync"}
        for eng in self.ENGS:
            myops = [o for o in self.ops if o.eng == eng]
            if not myops:
                continue

            def body(e, myops=myops, eng=eng):
                waited = {}
                for o in myops:
                    need = {}
                    for d, kind in o.deps.values():
                        if not self._needed(o, d, kind):
                            continue
                        if d.dma is not None:
                            key = ("d", d.dma)
                            val = 16 * d.dmacount
                        else:
                            key = ("e", d.eng)
                            val = d.semval
                        if val > need.get(key, 0):
                            need[key] = val
                    for key, val in need.items():
                        if waited.get(key, 0) >= val:
                            continue
                        waited[key] = val
                        sem = dma_sems[key[1]] if key[0] == "d" else sems[key[1]]
                        e.wait_ge(sem, val)
                    inst = o.fn(e)
                    if o.dma is not None:
                        inst.then_inc(dma_sems[o.dma], 16)
                    elif o.milestone:
                        inst.then_inc(sems[eng], 1)

            getattr(block, engobj[eng])(body)


def build(NT=NTILES, dbg=False, limit1=None, skip2=False, limit2=None):
    nc = bass.Bass("TRN2", target_bir_lowering=False, dynamic_dma_scratch_size=DMA_SCRATCH)
    NG2 = (NT - 1 + 3) // 4
    NOUT = (NT - 1) * 128

    def din(name, shape, dt=F32):
        return nc.dram_tensor(name, list(shape), dt, kind="ExternalInput").ap()

    x = din("x", [SEQ, D])
    meta = din("meta", [N_META, D])
    ln_emb_g = din("ln_emb_g", [1, D])
    ln_emb_b = din("ln_emb_b", [1, D])
    w_in = din("w_in", [D, IN_W])
    hg_lb = din("hg_lb", [1, 1024])
    hg_ng = din("hg_ng", [1, 128])
    sinks = din("sinks", [1, 8])
    w_bhg = din("w_bhg", [512, D])
    w_batt = din("w_batt", [512, D])
    w_out = din("w_out", [D, D])
    ln1_g = din("ln1_g", [1, D])
    ln1_b = din("ln1_b", [1, D])
    w_fi = din("w_fi", [D, 2 * D_FF])
    w_fo = din("w_fo", [D_FF, D])
    ln2_g = din("ln2_g", [1, D])
    ln2_b = din("ln2_b", [1, D])
    rope_t = din("rope_t", [NTILES * 128, 128])
    cmat = din("cmat", [128, 3 * 128])
    out = nc.dram_tensor("y_out", [NOUT, D], F32, kind="ExternalOutput").ap()
    h1_d = nc.dram_tensor("h1_scr", [NTILES * 128, D], F32, kind="Internal").ap()
    h1T_d = nc.dram_tensor("h1T_scr", [NTILES, 128, 1024], BF16, kind="Internal").ap()
    wfi_bf = nc.dram_tensor("wfi_bf", [D, 2 * D_FF], BF16, kind="Internal").ap()
    wfo_bf = nc.dram_tensor("wfo_bf", [D_FF, D], BF16, kind="Internal").ap()

    es = ExitStack()
    with es:
        def sb(name, shape, dt):
            return es.enter_context(nc.sbuf_tensor(name, list(shape), dt))

        sem_names = list(Sched.ENGS)
        sems = {e: es.enter_context(nc.semaphore("s_" + e)) for e in sem_names}
        sems2 = {e: es.enter_context(nc.semaphore("t_" + e)) for e in sem_names}
        fence_sem = es.enter_context(nc.semaphore("fence"))
        dma_sem_store = {}

        def dsem(name):
            if name not in dma_sem_store:
                dma_sem_store[name] = es.enter_context(nc.semaphore("d_" + name))
            return dma_sem_store[name]

        ps = es.enter_context(nc.psum_tensor("ps", [128, 8, 512], F32))
        block = es.enter_context(nc.Block())

        S = Sched()
        p1 = ExitStack()
        with p1:
            def sb1(name, shape, dt):
                return p1.enter_context(nc.sbuf_tensor(name, list(shape), dt))

            win_sb = sb1("win", [128, 8, IN_W], BF16)
            wbh_sb = sb1("wbh", [128, 4, D], BF16)
            wba_sb = sb1("wba", [128, 4, D], BF16)
            wo_sb = sb1("wo", [128, 8, D], BF16)
            B_win = [Buf("win%d" % c) for c in range(5)]
            B_wbh, B_wba, B_wo = Buf("wbh"), Buf("wba"), Buf("wo")
            Gemb = sb1("Gemb", [128, D], F32); Bemb = sb1("Bemb", [128, D], F32)
            G1 = sb1("G1", [128, D], F32); B1 = sb1("B1", [128, D], F32)
            C2 = sb1("C2", [128, 512], F32)
            ngcol = sb1("ngcol", [128, 1], F32)
            cm = sb1("cm", [128, 3, 128], F32)
            mask64 = sb1("mask64", [128, 128], BF16)
            ident = sb1("ident", [128, 128], BF16)
            esink = sb1("esink", [128, 8], F32)
            cneg = sb1("cneg", [128, 8], F32)
            vmask = sb1("vmask", [128, 1], F32)
            B_const = Buf("const")
            def dbl(name, shape, dt, n=2):
                return ([sb1("%s_%d" % (name, i), shape, dt) for i in range(n)],
                        [Buf("%s_%d" % (name, i)) for i in range(n)])

            x32, Bx32 = dbl("x32", [128, D], F32)
            hA, BhA = dbl("hA", [128, D], F32, 3)
            rt, Brt = dbl("rt", [128, 128], F32)
            stat = []
            for i in range(2):
                stat.append(dict(st6=sb1("st6_%d" % i, [128, 2, 6], F32), Bst6=Buf("st6"),
                                 mv=sb1("mv_%d" % i, [128, 2], F32), Bmv=Buf("mv"),
                                 ve=sb1("ve_%d" % i, [128, 1], F32), Bve=Buf("ve"),
                                 rs=sb1("rs_%d" % i, [128, 1], F32), Brs=Buf("rs")))
            hbf = sb1("hbf", [128, D], BF16); Bhbf = Buf("hbf")
            hT, BhT = dbl("hT", [128, 8, 128], BF16, 3)
            q2 = sb1("q2", [128, 512], F32); Bq2 = Buf("q2")
            k32 = sb1("k32", [128, 512], F32); Bk32 = Buf("k32")
            lnf = sb1("lnf", [128, 512], F32); Blnf = Buf("lnf")
            eb = [sb1("eb%d" % i, [128, 512], F32) for i in range(3)]
            Beb = [Buf("eb%d" % i) for i in range(3)]
            qt = sb1("qt", [128, 512], BF16); Bqt = Buf("qt")
            kt = sb1("kt", [128, 512], BF16); Bkt = Buf("kt")
            thq, Bthq, thf, Bthf = qt, Bqt, kt, Bkt
            kb, Bkb = dbl("kb", [128, 512], BF16)
            qkT, BqkT = dbl("qkT", [128, 8, 128], BF16)
            A_sb = sb1("A_sb", [128, 4, 128], BF16); BA_sb = Buf("A_sb")
            v_bf, Bv_bf = dbl("v_bf", [128, 512], BF16)
            gn2, Bgn2 = dbl("gn2", [128, 512], F32)
            ogs, Bogs = dbl("og", [128, 512], BF16)
            oT = sb1("oT", [128, 4, 128], BF16); BoT = Buf("oT")
            S32 = sb1("S32", [128, 4, 128], F32); BS32 = Buf("S32")
            Sbf = sb1("Sbf", [128, 4, 128], BF16); BSbf = Buf("Sbf")
            dec, Bdec = dbl("dec", [128, 8], F32)
            ss = sb1("ss", [128, 4], F32); Bss = Buf("ss")
            rsn = sb1("rsn", [128, 4], F32); Brsn = Buf("rsn")
            qr = sb1("qr", [128, 512], BF16); Bqr = Buf("qr")
            thg, Bthg = qr, Bqr
            kr = sb1("kr", [128, 128], BF16); Bkr = Buf("kr")
            qT, BqT = dbl("qT", [128, 4, 128], BF16)
            kTb, BkT = dbl("kT", [128, 128], BF16, 3)
            kTm = sb1("kTm", [128, 16], BF16); BkTm = Buf("kTm")
            vaug, Bvaug = dbl("vaug", [128, 2, 65], BF16, 3)
            vmeta = sb1("vmeta", [16, 2, 65], BF16); Bvmeta = Buf("vmeta")
            Ecur = sb1("Ecur", [128, 2, 512], BF16); BEcur = Buf("Ecur")
            Eprev = sb1("Eprev", [128, 2, 512], BF16); BEprev = Buf("Eprev")
            Emeta = sb1("Emeta", [16, 2, 512], BF16); BEmeta = Buf("Emeta")
            den = sb1("den", [128, 8], F32); Bden = Buf("den")
            rden = sb1("rden", [128, 8], F32); Brden = Buf("rden")
            aos, Baos = dbl("ao", [128, 512], BF16)
            thG = sb1("thG", [128, 2048], BF16); BthG = Buf("thG")
            m1 = sb1("m1", [128, D], BF16); Bm1 = Buf("m1")
            mm = sb1("mm", [128, D], BF16); Bmm = Buf("mm")
            mT = sb1("mT", [128, 8, 128], BF16); BmT = Buf("mT")
            r32 = sb1("r32", [128, D], F32); Br32 = Buf("r32")
            h1T = sb1("h1T", [128, 8, 128], BF16); Bh1T = Buf("h1T")

            Bbank = [Buf("bank%d" % i) for i in range(8)]
            rr = {"a": 0, "b": 0, "p": 0}

            def bankA():
                i = rr["a"]
                rr["a"] = (i + 1) % 3
                return i

            B1SEQ = [3, 4, 5, 4]
            B1CYC = [3, 5, 4]

            def bankB1():
                i = rr["b"]
                rr["b"] = i + 1
                return B1SEQ[i] if i < 4 else B1CYC[(i - 4) % 3]

            def bankB2():
                i = rr["p"]
                rr["p"] = (i + 1) % 2
                return 6 + i

            def pair():
                return 6

            def psb(i):
                return ps[:, i, :]

            def psb16(i):
                return ps[:, i, :].bitcast(BF16)

            setup_ops = []
            S.cur = setup_ops

            def bc_load(dst, src_row, n):
                return lambda e: e.dma_start(out=dst, in_=src_row.partition_broadcast(128))

            S.op("sp", lambda e: e.dma_start(out=cm[:].rearrange("p a b -> p (a b)"), in_=cmat),
                 writes=[B_const], dma="c0")
            S.op("sp", bc_load(Gemb[:], ln_emb_g, D), writes=[B_const], dma="c1")
            S.op("sp", bc_load(Bemb[:], ln_emb_b, D), writes=[B_const], dma="c2")
            S.op("sp", bc_load(G1[:], ln1_g, D), writes=[B_const], dma="c3")
            S.op("sp", bc_load(B1[:], ln1_b, D), writes=[B_const], dma="c4")
            S.op("sp", bc_load(eb[0][:], hg_lb[:, 0:512], 512), writes=[Beb[0]], dma="c5a")
            S.op("sp", bc_load(eb[1][:], hg_lb[:, 512:1024], 512), writes=[Beb[1]], dma="c5b")
            S.op("sp", bc_load(esink[:], sinks, 8), writes=[B_const], dma="c6")
            S.op("sp", lambda e: e.dma_start(out=ngcol[:], in_=hg_ng.rearrange("a p -> p a")), writes=[B_const],
                 dma="c7")
            S.op("dve", lambda e: e.tensor_copy(out=mask64[:], in_=cm[:, 0, :]), reads=[B_const], writes=[B_const])
            S.op("dve", lambda e: e.tensor_copy(out=ident[:], in_=cm[:, 2, :]), reads=[B_const], writes=[B_const])
            S.op("dve", lambda e: e.memset(cneg[:], -0.5), writes=[B_const])
            S.op("dve", lambda e: e.memset(vmask[:], 1.0), writes=[B_const])
            S.op("pool", lambda e: e.affine_select(out=vmask[:], in_=vmask[:], pattern=[[0, 1]],
                                                   compare_op=ALU.is_ge, fill=0.0, base=-PAD,
                                                   channel_multiplier=1),
                 reads=[B_const], writes=[B_const])
            S.op("dve", lambda e: e.tensor_tensor(out=C2[:], in0=eb[0][:], in1=eb[1][:], op=ALU.subtract),
                 reads=[Beb[0], Beb[1]], writes=[B_const])
            S.op("act", lambda e: e.activation(out=C2[:], in_=C2[:], func=AF.Tanh, scale=0.5),
                 reads=[B_const], writes=[B_const])
            S.op("dve", lambda e: e.tensor_scalar(out=C2[:], in0=C2[:], scalar1=-0.25, scalar2=0.25,
                                                  op0=ALU.mult, op1=ALU.add),
                 reads=[B_const], writes=[B_const])
            S.op("act", lambda e: e.activation(out=esink[:], in_=esink[:], func=AF.Exp),
                 reads=[B_const], writes=[B_const])
            S.op("dve", lambda e: e.memset(S32[:], 0.0), writes=[BS32])
            S.op("dve", lambda e: e.memset(Sbf[:], 0.0), writes=[BSbf])
            for i in range(3):
                S.op("dve", (lambda e, i=i: e.memset(vaug[i][:], 1.0)), writes=[Bvaug[i]])

            stage1 = [(x32[0], Bx32[0]), (x32[1], Bx32[1]), (hA[0], BhA[0]), (hA[1], BhA[1]), (r32, Br32)]
            wl = {"i": 0}

            def load_w(Sx, stages, dst_ap, src_ap, n, Bdst, tag, scale_col=None):
                k = wl["i"]
                wl["i"] += 1
                stg, Bstg = stages[k % len(stages)]
                ce = ("act", "dve")[k % 2]
                Sx.op("sp", lambda e: e.dma_start(out=stg[:, 0:n], in_=src_ap), writes=[Bstg],
                      dma="%s%d" % (tag, k % len(stages)))
                if scale_col is not None:
                    Sx.op("dve", lambda e: e.tensor_scalar(out=dst_ap, in0=stg[:, 0:n], scalar1=scale_col[:, 0:1],
                                                           scalar2=0.5, op0=ALU.mult, op1=ALU.mult),
                          reads=[Bstg, B_const], writes=[Bdst])
                elif ce == "act":
                    Sx.op("act", lambda e: e.copy(out=dst_ap, in_=stg[:, 0:n]), reads=[Bstg], writes=[Bdst])
                else:
                    Sx.op(ce, lambda e: e.tensor_copy(out=dst_ap, in_=stg[:, 0:n]), reads=[Bstg], writes=[Bdst])

            w_in_v = w_in.rearrange("(c p) n -> p c n", p=128)
            for col0 in range(0, IN_W, 1024):
                for c in range(8):
                    n = min(1024, IN_W - col0)
                    load_w(S, stage1, win_sb[:, c, col0:col0 + n], w_in_v[:, c, col0:col0 + n], n,
                           B_win[col0 // 1024], "ws")
            wbh_v = w_bhg.rearrange("(c p) n -> p c n", p=128)
            wba_v = w_batt.rearrange("(c p) n -> p c n", p=128)
            wo_v = w_out.rearrange("(c p) n -> p c n", p=128)
            for c in range(4):
                load_w(S, stage1, wbh_sb[:, c, :], wbh_v[:, c, :], 1024, B_wbh, "ws", scale_col=ngcol)
            for c in range(4):
                load_w(S, stage1, wba_sb[:, c, :], wba_v[:, c, :], 1024, B_wba, "ws")
            for c in range(8):
                load_w(S, stage1, wo_sb[:, c, :], wo_v[:, c, :], 1024, B_wo, "ws")

            cvB = [Buf("cv%d" % i) for i in range(2)]
            ncv = 0
            cv_ops = []
            S.cur = cv_ops
            for r0 in range(0, D, 128):
                S.op("pool", (lambda e, r0=r0: e.dma_start(out=wfi_bf[r0:r0 + 128, :], in_=w_fi[r0:r0 + 128, :])),
                     reads=[B_wo], writes=[cvB[ncv % 2]], dma="cv%d" % (ncv % 2))
                ncv += 1
            for r0 in range(0, D_FF, 128):
                S.op("pool", (lambda e, r0=r0: e.dma_start(out=wfo_bf[r0:r0 + 128, :], in_=w_fo[r0:r0 + 128, :])),
                     writes=[cvB[ncv % 2]], dma="cv%d" % (ncv % 2))
                ncv += 1

            S.cur = setup_ops
            T64 = cm[:, 0, :]
            U64 = cm[:, 1, :]

            def transposes(src_fn, n, dst2, BdstT, Bsrc, evac_eng, bk):
                pv = psb16(bk)

                def pe_fn(e):
                    inst = None
                    for i in range(n):
                        inst = e.transpose(pv[:, i * 128:(i + 1) * 128], src_fn(i), ident[:])
                    return inst
                S.op("pe", pe_fn, reads=[Bsrc, B_const], writes=[Bbank[bk]])
                if evac_eng == "act":
                    S.op("act", lambda e: e.copy(out=dst2, in_=pv[:, 0:n * 128]), reads=[Bbank[bk]], writes=[BdstT])
                else:
                    S.op("dve", lambda e: e.tensor_copy(out=dst2, in_=pv[:, 0:n * 128]), reads=[Bbank[bk]],
                         writes=[BdstT])

            def flat3(tl):
                return tl[:].rearrange("p a b -> p (a b)")

            def proj(hTt, BhTt, col0, ncols, bk):
                def pe_fn(e):
                    inst = None
                    for c in range(8):
                        inst = e.matmul(psb(bk)[:, 0:ncols], hTt[:, c, :], win_sb[:, c, col0:col0 + ncols],
                                        start=(c == 0), stop=(c == 7))
                    return inst
                S.op("pe", pe_fn, reads=[BhTt] + B_win[col0 // 1024:(col0 + ncols - 1) // 1024 + 1],
                     writes=[Bbank[bk]])
                return bk

            def ln_stats(sd, src, Bsrc, dst, Bdst, Gt, Bt, eps):
                st6, mv, ve, rs = sd["st6"], sd["mv"], sd["ve"], sd["rs"]

                def f(e):
                    e.bn_stats(out=st6[:, 0, :], in_=src[:, 0:512])
                    return e.bn_stats(out=st6[:, 1, :], in_=src[:, 512:1024])
                S.op("dve", f, reads=[Bsrc], writes=[sd["Bst6"]])
                S.op("dve", lambda e: e.bn_aggr(out=mv[:], in_=st6[:].rearrange("p a b -> p (a b)")),
                     reads=[sd["Bst6"]], writes=[sd["Bmv"]])
                S.op("pool", lambda e: e.tensor_scalar(out=ve[:], in0=mv[:, 1:2], scalar1=eps, scalar2=None,
                                                       op0=ALU.add), reads=[sd["Bmv"]], writes=[sd["Bve"]])
                S.op("pool", lambda e: e.tensor_tensor(out=rs[:], in0=ve[:], in1=cneg[:, 0:1], op=ALU.pow),
                     reads=[sd["Bve"], B_const], writes=[sd["Brs"]])
                S.op("dve", lambda e: e.scalar_tensor_tensor(out=src[:], in0=src[:], scalar=mv[:, 0:1],
                                                             in1=Gt[:], op0=ALU.subtract, op1=ALU.mult),
                     reads=[Bsrc, sd["Bmv"], B_const], writes=[Bsrc])
                S.op("dve", lambda e: e.scalar_tensor_tensor(out=dst[:], in0=src[:], scalar=rs[:, 0:1],
                                                             in1=Bt[:], op0=ALU.mult, op1=ALU.add),
                     reads=[Bsrc, sd["Brs"], B_const] + ([Bdst] if Bdst is not Bsrc else []), writes=[Bdst])

            def load_tile(t):
                i2 = t % 2
                xb, Bxb = x32[i2], Bx32[i2]
                if t == 0:
                    S.op("pool", lambda e: e.memset(xb[:], 0.0), writes=[Bxb])
                    S.op("sp", lambda e: e.dma_start(out=xb[PAD:128, :], in_=meta), reads=[Bxb], writes=[Bxb],
                         dma="x0")
                else:
                    S.op("sp", lambda e: e.dma_start(out=xb[:], in_=x[(t - 1) * 128:t * 128, :]),
                         writes=[Bxb], dma="x%d" % i2)
                S.op("sp", lambda e: e.dma_start(out=rt[i2][:], in_=rope_t[t * 128:(t + 1) * 128, :]),
                     writes=[Brt[i2]], dma="rt%d" % i2)

            def do_A(t):
                i2, i3 = t % 2, t % 3
                xb, Bxb = x32[i2], Bx32[i2]
                hTt, BhTt = hT[i3], BhT[i3]
                if t + 1 < NT:
                    load_tile(t + 1)
                ln_stats(stat[0], xb, Bxb, hA[i3], BhA[i3], Gemb, Bemb, EPS)
                if t == 0:
                    S.op("dve", lambda e: e.tensor_scalar(out=hA[i3][:], in0=hA[i3][:], scalar1=vmask[:, 0:1],
                                                          scalar2=None, op0=ALU.mult),
                         reads=[BhA[i3], B_const], writes=[BhA[i3]])
                S.op("act", lambda e: e.copy(out=hbf[:], in_=hA[i3][:]), reads=[BhA[i3]], writes=[Bhbf])
                transposes(lambda i: hbf[:, i * 128:(i + 1) * 128], 8, flat3(hTt), BhTt, Bhbf, "act", bankA())

                bq = proj(hTt, BhTt, 0, 512, bankA())
                S.op("act", lambda e: e.activation(out=thq[:], in_=psb(bq), func=AF.Tanh, scale=0.5),
                     reads=[Bbank[bq]], writes=[Bthq])
                S.op("dve", lambda e: e.scalar_tensor_tensor(out=q2[:], in0=thq[:], scalar=1.0, in1=psb(bq),
                                                             op0=ALU.add, op1=ALU.mult),
                     reads=[Bthq, Bbank[bq]], writes=[Bq2])
                bf_ = proj(hTt, BhTt, 512, 512, bankA())
                S.op("act", lambda e: e.activation(out=thf[:], in_=psb(bf_), func=AF.Tanh, scale=-0.5),
                     reads=[Bbank[bf_]], writes=[Bthf])
                S.op("dve", lambda e: e.scalar_tensor_tensor(out=k32[:], in0=thf[:], scalar=1.0, in1=C2[:],
                                                             op0=ALU.add, op1=ALU.mult),
                     reads=[Bthf, B_const], writes=[Bk32])
                if t == 0:
                    S.op("dve", lambda e: e.tensor_scalar(out=k32[:], in0=k32[:], scalar1=vmask[:, 0:1], scalar2=None,
                                                          op0=ALU.mult), reads=[Bk32, B_const], writes=[Bk32])
                bv = proj(hTt, BhTt, 1024, 512, bankA())
                S.op("act", lambda e: e.copy(out=v_bf[i2][:], in_=psb(bv)), reads=[Bbank[bv]], writes=[Bv_bf[i2]])
                bg = proj(hTt, BhTt, 1536, 512, bankA())
                S.op("act", lambda e: e.activation(out=thg[:], in_=psb(bg), func=AF.Tanh, scale=0.5),
                     reads=[Bbank[bg]], writes=[Bthg])
                S.op("dve", lambda e: e.scalar_tensor_tensor(out=gn2[i2][:], in0=thg[:], scalar=1.0, in1=psb(bg),
                                                             op0=ALU.add, op1=ALU.mult),
                     reads=[Bthg, Bbank[bg]], writes=[Bgn2[i2]])

                S.op("act", lambda e: e.activation(out=lnf[:], in_=k32[:], func=AF.Ln, scale=-1.0, bias=1.0),
                     reads=[Bk32], writes=[Blnf])
                bP = bankA()
                S.op("pe", lambda e: e.matmul(psb(bP), T64, lnf[:], start=True, stop=True),
                     reads=[Blnf, B_const], writes=[Bbank[bP]])
                bS = bankA()
                S.op("pe", lambda e: e.matmul(psb(bS), U64, lnf[:], start=True, stop=True),
                     reads=[Blnf, B_const], writes=[Bbank[bS]])
                bD = bankA()

                def dec_fn(e):
                    inst = None
                    for hh in range(4):
                        inst = e.matmul(psb(bD)[:, 2 * hh:2 * hh + 2], lnf[:, hh * 128:(hh + 1) * 128],
                                        cm[:, 0, 63:128:64], start=True, stop=True)
                    return inst
                S.op("pe", dec_fn, reads=[Blnf, B_const], writes=[Bbank[bD]])
                S.op("act", lambda e: e.activation(out=eb[0][:], in_=psb(bP), func=AF.Exp, bias=math.log(0.5)),
                     reads=[Bbank[bP]], writes=[Beb[0]])
                S.op("dve", lambda e: e.tensor_tensor(out=qt[:], in0=q2[:], in1=eb[0][:], op=ALU.mult),
                     reads=[Bq2, Beb[0]], writes=[Bqt])
                S.op("act", lambda e: e.activation(out=eb[1][:], in_=psb(bP), func=AF.Exp, scale=-1.0),
                     reads=[Bbank[bP]], writes=[Beb[1]])
                S.op("dve", lambda e: e.tensor_tensor(out=kt[:], in0=k32[:], in1=eb[1][:], op=ALU.mult),
                     reads=[Bk32, Beb[1]], writes=[Bkt])
                S.op("act", lambda e: e.activation(out=eb[2][:], in_=psb(bS), func=AF.Exp),
                     reads=[Bbank[bS]], writes=[Beb[2]])
                S.op("dve", lambda e: e.tensor_tensor(out=kb[i2][:], in0=k32[:], in1=eb[2][:], op=ALU.mult),
                     reads=[Bk32, Beb[2]], writes=[Bkb[i2]])
                S.op("act", lambda e: e.activation(out=dec[i2][:], in_=psb(bD)[:, 0:8], func=AF.Exp),
                     reads=[Bbank[bD]], writes=[Bdec[i2]])

                bqk = bankA()
                pvq = psb16(bqk)

                def qk_tr(e):
                    inst = None
                    for hh in range(4):
                        inst = e.transpose(pvq[:, hh * 128:(hh + 1) * 128], qt[:, hh * 128:(hh + 1) * 128], ident[:])
                    for hh in range(4):
                        inst = e.transpose(pvq[:, (4 + hh) * 128:(5 + hh) * 128], kt[:, hh * 128:(hh + 1) * 128],
                                           ident[:])
                    return inst
                S.op("pe", qk_tr, reads=[Bqt, Bkt, B_const], writes=[Bbank[bqk]])
                S.op("dve", lambda e: e.tensor_copy(out=flat3(qkT[i2]), in_=pvq), reads=[Bbank[bqk]],
                     writes=[BqkT[i2]])

                R1 = rt[i2][:, 0:64]
                R2a = rt[i2][:, 64:96]
                R2b = rt[i2][:, 96:128]

                def dst_in(tb, nh):
                    if nh == 8:
                        return tb[:].rearrange("p (a j d) -> p a j d", a=2, j=4)
                    return tb[:, 0:128]

                def rope(bk_, nh, dst_ap, Bdst, t1, Bt1, t2, Bt2):
                    src = psb(bk_)[:, 0:nh * 64].rearrange("p (h d) -> p h d", d=64)
                    t1v = t1[:, 0:nh * 64].rearrange("p (h d) -> p h d", d=64)
                    t2v = t2[:, 0:nh * 64].rearrange("p (h d) -> p h d", d=64)
                    S.op("dve", lambda e: e.tensor_tensor(out=t1v, in0=src,
                                                          in1=R1.unsqueeze(1).to_broadcast([128, nh, 64]),
                                                          op=ALU.mult),
                         reads=[Bbank[bk_], Brt[i2]], writes=[Bt1])

                    def f2(e):
                        e.tensor_tensor(out=t2v[:, :, 0:32], in0=src[:, :, 32:64],
                                        in1=R2a.unsqueeze(1).to_broadcast([128, nh, 32]), op=ALU.mult)
                        return e.tensor_tensor(out=t2v[:, :, 32:64], in0=src[:, :, 0:32],
                                               in1=R2b.unsqueeze(1).to_broadcast([128, nh, 32]), op=ALU.mult)
                    S.op("dve", f2, reads=[Bbank[bk_], Brt[i2]], writes=[Bt2])
                    S.op("dve", lambda e: e.tensor_tensor(out=dst_ap, in0=dst_in(t1, nh), in1=dst_in(t2, nh),
                                                          op=ALU.add),
                         reads=[Bt1, Bt2], writes=[Bdst])

                bakv = proj(hTt, BhTt, 2560, 256, bankA())
                rope(bakv, 2, kr[:], Bkr, eb[0], Beb[0], eb[1], Beb[1])
                S.op("dve", lambda e: e.tensor_copy(
                    out=vaug[i3][:, :, 0:64], in_=psb(bakv)[:, 128:256].rearrange("p (g d) -> p g d", d=64)),
                     reads=[Bbank[bakv]], writes=[Bvaug[i3]])
                bkT = bankA()
                pvk = psb16(bkT)
                S.op("pe", lambda e: e.transpose(pvk[:, 0:128], kr[:], ident[:]),
                     reads=[Bkr, B_const], writes=[Bbank[bkT]])
                S.op("dve", lambda e: e.tensor_copy(out=kTb[i3][:], in_=pvk[:, 0:128]),
                     reads=[Bbank[bkT]], writes=[BkT[i3]])
                if t == 0:
                    S.op("dve", lambda e: e.tensor_copy(out=kTm[:], in_=kTb[0][:, PAD:128]), reads=[BkT[0]],
                         writes=[BkTm])
                    S.op("sp", lambda e: e.dma_start(out=vmeta[:], in_=vaug[0][PAD:128, :, :]), reads=[Bvaug[0]],
                         writes=[Bvmeta], dma="vmeta")
                    return
                baq = proj(hTt, BhTt, 2048, 512, bankA())
                rope(baq, 8, qr[:].rearrange("p (j a d) -> p a j d", a=2, d=64), Bqr, eb[2], Beb[2], eb[0], Beb[0])
                transposes(lambda i: qr[:, i * 128:(i + 1) * 128], 4, flat3(qT[i2]), BqT[i2], Bqr, "dve", bankA())

            def do_B1(t):
                i2, i3, p3 = t % 2, t % 3, (t - 1) % 3
                rr["b"] = 0
                bankB = bankB1
                og, Bog, ao, Bao = ogs[i2], Bogs[i2], aos[i2], Baos[i2]
                qk, Bqk = qkT[i2], BqkT[i2]
                vb, Bvb = v_bf[i2], Bv_bf[i2]
                kbt, Bkbt = kb[i2], Bkb[i2]
                dct, Bdct = dec[i2], Bdec[i2]
                bO = bankB()
                if t > 0:
                    bA = bankB()

                    def a_fn(e):
                        inst = None
                        for hh in range(4):
                            inst = e.matmul(psb(bA)[:, hh * 128:(hh + 1) * 128], qk[:, 4 + hh, :], qk[:, hh, :],
                                            start=True, stop=True)
                        return inst
                    S.op("pe", a_fn, reads=[Bqk], writes=[Bbank[bA]])
                    S.op("dve", lambda e: e.tensor_tensor(
                        out=A_sb[:], in0=psb(bA).rearrange("p (h t) -> p h t", h=4),
                        in1=mask64[:].unsqueeze(1).to_broadcast([128, 4, 128]), op=ALU.mult),
                         reads=[Bbank[bA], B_const], writes=[BA_sb])

                    def o1_fn(e):
                        inst = None
                        for hh in range(4):
                            inst = e.matmul(psb(bO)[:, hh * 128:(hh + 1) * 128], A_sb[:, hh, :],
                                            vb[:, hh * 128:(hh + 1) * 128], start=(hh == 0), stop=False,
                                            skip_group_check=True)
                        for hh in range(4):
                            inst = e.matmul(psb(bO)[0:64, hh * 128:(hh + 1) * 128], qk[:, hh, 0:64], Sbf[:, hh, :],
                                            start=False, stop=False, skip_group_check=True)
                        return inst
                    S.op("pe", o1_fn, reads=[BA_sb, Bvb, Bqk, BSbf], writes=[Bbank[bO]])
                bSa = bankB()
                bSb = bankB()

                def st_fn(e):
                    inst = None
                    for hh in range(4):
                        inst = e.matmul(psb(bSa)[:, hh * 128:(hh + 1) * 128], kbt[0:64, hh * 128:(hh + 1) * 128],
                                        vb[0:64, hh * 128:(hh + 1) * 128], start=True, stop=True)
                    for hh in range(4):
                        inst = e.matmul(psb(bSb)[:, hh * 128:(hh + 1) * 128], kbt[64:128, hh * 128:(hh + 1) * 128],
                                        vb[64:128, hh * 128:(hh + 1) * 128], start=True, stop=True)
                    return inst
                S.op("pe", st_fn, reads=[Bkbt, Bvb], writes=[Bbank[bSa], Bbank[bSb]])

                def upd(e, b, col):
                    inst = None
                    for hh in range(4):
                        inst = e.scalar_tensor_tensor(out=S32[:, hh, :], in0=S32[:, hh, :],
                                                      scalar=dct[:, 2 * hh + col:2 * hh + col + 1],
                                                      in1=psb(b)[:, hh * 128:(hh + 1) * 128],
                                                      op0=ALU.mult, op1=ALU.add)
                    return inst
                S.op("dve", lambda e: upd(e, bSa, 0), reads=[BS32, Bdct, Bbank[bSa], BSbf], writes=[BS32])
                S.op("act", lambda e: e.copy(out=flat3(Sbf), in_=flat3(S32)), reads=[BS32], writes=[BSbf])
                if t > 0:
                    def o2_fn(e):
                        inst = None
                        for hh in range(4):
                            inst = e.matmul(psb(bO)[64:128, hh * 128:(hh + 1) * 128], qk[:, hh, 64:128], Sbf[:, hh, :],
                                            start=False, stop=(hh == 3), skip_group_check=True)
                        return inst
                    S.op("pe", o2_fn, reads=[Bqk, BSbf, Bbank[bO]], writes=[Bbank[bO]])
                S.op("dve", lambda e: upd(e, bSb, 1), reads=[BS32, Bdct, Bbank[bSb], BSbf], writes=[BS32])
                S.op("act", lambda e: e.copy(out=flat3(Sbf), in_=flat3(S32)), reads=[BS32], writes=[BSbf])
                if t == 0:
                    return

                def sq_fn(e):
                    inst = None
                    for hh in range(4):
                        inst = e.activation(out=og[:, hh * 128:(hh + 1) * 128], in_=psb(bO)[:, hh * 128:(hh + 1) * 128],
                                            func=AF.Square, accum_out=ss[:, hh:hh + 1])
                    return inst
                S.op("act", sq_fn, reads=[Bbank[bO]], writes=[Bss, Bog])
                S.op("pool", lambda e: e.tensor_scalar(out=rsn[:], in0=ss[:], scalar1=1.0 / 128, scalar2=EPS,
                                                       op0=ALU.mult, op1=ALU.add), reads=[Bss], writes=[Brsn])
                S.op("pool", lambda e: e.tensor_tensor(out=rsn[:], in0=rsn[:], in1=cneg[:, 0:4], op=ALU.pow),
                     reads=[Brsn, B_const], writes=[Brsn])

                def og_fn(e):
                    inst = None
                    for hh in range(4):
                        inst = e.scalar_tensor_tensor(out=og[:, hh * 128:(hh + 1) * 128],
                                                      in0=psb(bO)[:, hh * 128:(hh + 1) * 128],
                                                      scalar=rsn[:, hh:hh + 1],
                                                      in1=gn2[i2][:, hh * 128:(hh + 1) * 128],
                                                      op0=ALU.mult, op1=ALU.mult)
                    return inst
                S.op("dve", og_fn, reads=[Bbank[bO], Brsn, Bgn2[i2], Bog], writes=[Bog])

                qT2 = flat3(qT[i2])

                def scores(kT_ap, nkeys, E, BE, Bk, mask):
                    bs0, bs1 = bankB(), bankB()

                    def f(e):
                        e.matmul(psb(bs0)[0:nkeys, :], kT_ap[0:64, 0:nkeys], qT2[0:64, :], start=True, stop=True)
                        return e.matmul(psb(bs1)[0:nkeys, :], kT_ap[64:128, 0:nkeys], qT2[64:128, :], start=True,
                                        stop=True)
                    S.op("pe", f, reads=[Bk, BqT[i2]], writes=[Bbank[bs0], Bbank[bs1]])
                    S.op("act", lambda e: e.activation(out=E[0:nkeys, 0, :], in_=psb(bs0)[0:nkeys, :], func=AF.Exp,
                                                       scale=0.125), reads=[Bbank[bs0]], writes=[BE])
                    S.op("act", lambda e: e.activation(out=E[0:nkeys, 1, :], in_=psb(bs1)[0:nkeys, :], func=AF.Exp,
                                                       scale=0.125), reads=[Bbank[bs1], BE], writes=[BE])
                    Ev = E[:].rearrange("p g (j q) -> p (g j) q", q=128)
                    if mask == "cur":
                        S.op("pool", lambda e: e.affine_select(
                            out=Ev, in_=Ev, pattern=[[0, 8], [1, 128]], compare_op=ALU.is_ge, fill=0.0, base=0,
                            channel_multiplier=-1), reads=[BE], writes=[BE])
                    elif mask == "prev":
                        S.op("pool", lambda e: e.affine_select(
                            out=Ev, in_=Ev, pattern=[[0, 8], [-1, 128]], compare_op=ALU.is_ge, fill=0.0, base=-1,
                            channel_multiplier=1), reads=[BE], writes=[BE])

                scores(kTb[i3], 128, Ecur, BEcur, BkT[i3], "cur")
                has_prev = t >= 2
                if has_prev:
                    scores(kTb[p3], 128, Eprev, BEprev, BkT[p3], "prev")
                scores(kTm, 16, Emeta, BEmeta, BkTm, None)
                bo = [bankB(), bankB()]

                def pv_fn(e):
                    inst = None
                    for h8 in range(8):
                        g, j = h8 // 4, h8 % 4
                        o_ap = psb(bo[g])[:, j * 65:(j + 1) * 65]
                        inst = e.matmul(o_ap, Ecur[:, g, j * 128:(j + 1) * 128], vaug[i3][:, g, :], start=True,
                                        stop=False)
                        if has_prev:
                            inst = e.matmul(o_ap, Eprev[:, g, j * 128:(j + 1) * 128], vaug[p3][:, g, :],
                                            start=False, stop=False)
                        inst = e.matmul(o_ap, Emeta[0:16, g, j * 128:(j + 1) * 128], vmeta[0:16, g, :], start=False,
                                        stop=True)
                    return inst
                rds = [BEcur, BEmeta, Bvaug[i3], Bvmeta] + ([BEprev, Bvaug[p3]] if has_prev else [])
                S.op("pe", pv_fn, reads=rds, writes=[Bbank[bo[0]], Bbank[bo[1]]])
                for g in range(2):
                    ov = psb(bo[g])[:, 0:260].rearrange("p (j d) -> p j d", d=65)
                    S.op("dve", (lambda e, ov=ov, g=g: e.tensor_tensor(
                        out=den[:, 4 * g:4 * g + 4].unsqueeze(2), in0=ov[:, :, 64:65],
                        in1=esink[:, 4 * g:4 * g + 4].unsqueeze(2), op=ALU.add)),
                         reads=[Bbank[bo[g]], B_const, Bden], writes=[Bden])
                S.op("dve", lambda e: e.reciprocal(out=rden[:], in_=den[:]), reads=[Bden], writes=[Brden])
                for g in range(2):
                    ov = psb(bo[g])[:, 0:260].rearrange("p (j d) -> p j d", d=65)
                    S.op("dve", (lambda e, ov=ov, g=g: e.tensor_tensor(
                        out=ao[:, 256 * g:256 * g + 256].rearrange("p (j d) -> p j d", d=64), in0=ov[:, :, 0:64],
                        in1=rden[:, 4 * g:4 * g + 4].unsqueeze(2).to_broadcast([128, 4, 64]), op=ALU.mult)),
                         reads=[Bbank[bo[g]], Brden, Bao], writes=[Bao])

            def do_B2(t):
                i2, i3 = t % 2, t % 3
                if t == 0:
                    return
                bankB = bankB2
                hTt, BhTt = hT[i3], BhT[i3]
                og, Bog, ao, Bao = ogs[i2], Bogs[i2], aos[i2], Baos[i2]
                for gi in range(4):
                    bgt = proj(hTt, BhTt, 2816 + gi * 512, 512, bankB())
                    S.op("act", (lambda e, b=bgt, gi=gi: e.activation(out=thG[:, gi * 512:(gi + 1) * 512],
                                                                      in_=psb(b), func=AF.Tanh, scale=0.5)),
                         reads=[Bbank[bgt]], writes=[BthG])

                def branch(src, Bsrc, w_sb, Bw):
                    transposes(lambda i: src[:, i * 128:(i + 1) * 128], 4, flat3(oT), BoT, Bsrc, "act", bankB())
                    pp = pair()

                    def f(e):
                        inst = None
                        for n in range(2):
                            for c in range(4):
                                inst = e.matmul(psb(pp + n), oT[:, c, :], w_sb[:, c, n * 512:(n + 1) * 512],
                                                start=(c == 0), stop=(c == 3))
                        return inst
                    S.op("pe", f, reads=[BoT, Bw], writes=[Bbank[pp], Bbank[pp + 1]])
                    return pp

                pp1 = branch(og, Bog, wbh_sb, B_wbh)
                for n in range(2):
                    S.op("dve", (lambda e, n=n: e.scalar_tensor_tensor(
                        out=m1[:, n * 512:(n + 1) * 512], in0=thG[:, n * 512:(n + 1) * 512], scalar=1.0,
                        in1=psb(pp1 + n), op0=ALU.add, op1=ALU.mult)),
                         reads=[BthG, Bbank[pp1 + n], Bm1], writes=[Bm1])
                pp2 = branch(ao, Bao, wba_sb, B_wba)
                for n in range(2):
                    S.op("dve", (lambda e, n=n: e.scalar_tensor_tensor(
                        out=r32[:, n * 512:(n + 1) * 512], in0=thG[:, 1024 + n * 512:1024 + (n + 1) * 512],
                        scalar=1.0, in1=psb(pp2 + n), op0=ALU.add, op1=ALU.mult)),
                         reads=[BthG, Bbank[pp2 + n], Br32], writes=[Br32])
                S.op("dve", lambda e: e.tensor_tensor(out=mm[:], in0=m1[:], in1=r32[:], op=ALU.add),
                     reads=[Bm1, Br32], writes=[Bmm])
                transposes(lambda i: mm[:, i * 128:(i + 1) * 128], 8, flat3(mT), BmT, Bmm, "act", bankB())
                pp3 = pair()

                def wo_fn(e):
                    inst = None
                    for n in range(2):
                        for c in range(8):
                            inst = e.matmul(psb(pp3 + n), mT[:, c, :], wo_sb[:, c, n * 512:(n + 1) * 512],
                                            start=(c == 0), stop=(c == 7))
                    return inst
                S.op("pe", wo_fn, reads=[BmT, B_wo], writes=[Bbank[pp3], Bbank[pp3 + 1]])
                for n in range(2):
                    S.op("dve", (lambda e, n=n: e.scalar_tensor_tensor(
                        out=r32[:, n * 512:(n + 1) * 512], in0=psb(pp3 + n), scalar=0.5 / ALPHA,
                        in1=hA[i3][:, n * 512:(n + 1) * 512], op0=ALU.mult, op1=ALU.add)),
                         reads=[Bbank[pp3 + n], BhA[i3], Br32], writes=[Br32])
                ln_stats(stat[1], r32, Br32, r32, Br32, G1, B1, EPS / (ALPHA * ALPHA))
                S.op("act", lambda e: e.copy(out=mm[:], in_=r32[:]), reads=[Br32], writes=[Bmm])
                S.op("sp", lambda e: e.dma_start(out=h1_d[t * 128:(t + 1) * 128, :], in_=r32[:]),
                     reads=[Br32], dma="h1o")
                transposes(lambda i: mm[:, i * 128:(i + 1) * 128], 8, flat3(h1T), Bh1T, Bmm, "act", bankB())
                S.op("sp", lambda e: e.dma_start(out=h1T_d[t], in_=flat3(h1T)), reads=[Bh1T], dma="h1To")

            def merge(la, lb):
                out_, ia, ib = [], 0, 0
                na, nb = len(la), len(lb)
                while ia < na or ib < nb:
                    if ib >= nb or (ia < na and ia * nb <= ib * na):
                        out_.append(la[ia]); ia += 1
                    else:
                        out_.append(lb[ib]); ib += 1
                return out_

            def merge3(ls):
                ls = [l for l in ls if l]
                if not ls:
                    return []
                out_ = ls[0]
                tot = len(ls[0])
                for l in ls[1:]:
                    out_ = merge(out_, l)
                return out_

            order = list(setup_ops)
            S.cur = lst = []
            load_tile(0)
            do_A(0)
            order += lst
            for t in range(NT + 1):
                S.cur = lB2 = []
                if t >= 1:
                    do_B2(t - 1)
                S.cur = lB1 = []
                if t < NT:
                    do_B1(t)
                S.cur = lA = []
                if t + 1 < NT:
                    do_A(t + 1)
                extra = []
                if t >= min(6, NT):
                    take = len(cv_ops) if t == NT else 3
                    extra, cv_ops[:] = cv_ops[:take], cv_ops[take:]
                order += merge3([lB2, lB1, lA, extra]) if PIPELINE else (lB2 + lB1 + lA + extra)
            S.ops = order

            if limit1 is not None:
                S.ops = S.ops[:limit1]
            S.finalize()
            S.emit(nc, block, sems, {k: dsem(k) for k in S.dma_counts})
            final = {e: 0 for e in Sched.ENGS}
            for o in S.ops:
                if o.dma is None and o.milestone:
                    final[o.eng] = max(final[o.eng], o.semval)
            dma_final = {k: 16 * v for k, v in S.dma_counts.items()}

            def sp_fence(e):
                for k, v in final.items():
                    if v > 0:
                        e.wait_ge(sems[k], v)
                for k, v in dma_final.items():
                    e.wait_ge(dsem(k), v)
                e.sem_inc(fence_sem, 1)
            block.sync(sp_fence)
            for en in ("tensor", "scalar", "vector", "gpsimd"):
                getattr(block, en)(lambda e: e.wait_ge(fence_sem, 1))

        S2 = Sched()
        p2 = ExitStack()
        with p2:
            def sb2(name, shape, dt):
                return p2.enter_context(nc.sbuf_tensor(name, list(shape), dt))

            NJ = D_FF // 128
            wfi_sb = sb2("wfi", [128, 8, 2 * D_FF], BF16)
            wfo_sb = sb2("wfo", [128, NJ, D], BF16)
            B_wfi = [[Buf("wfi%d_%d" % (b_, c)) for c in range(8)] for b_ in range(6)]
            B_wfo = [Buf("wfo%d" % j) for j in range(D_FF // 128)]
            G2 = sb2("G2", [128, D], F32); B2 = sb2("B2", [128, D], F32)
            cneg2 = sb2("cneg2", [128, 1], F32)
            Bc2 = Buf("c2")
            hTg = [sb2("hTg%d" % i, [128, 4, 8, 128], BF16) for i in range(2)]
            BhTg = [Buf("hTg%d" % i) for i in range(2)]
            h1in = [sb2("h1in%d" % i, [128, D], F32) for i in range(2)]
            Bh1in = [Buf("h1in%d" % i) for i in range(2)]
            gT = sb2("gT", [128, NJ, 512], BF16)
            BgT = [Buf("gT%d" % j) for j in range(NJ)]
            tha = [sb2("tha%d" % i, [128, 512], BF16) for i in range(2)]
            Btha = [Buf("tha%d" % i) for i in range(2)]
            s2b = [sb2("s2b%d" % i, [128, 512], F32) for i in range(2)]
            Bs2b = [Buf("s2b%d" % i) for i in range(2)]
            rr2 = [sb2("rr2_%d" % i, [128, D], F32) for i in range(2)]
            Brr2 = [Buf("rr2_%d" % i) for i in range(2)]
            st6b = sb2("st6b", [128, 2, 6], F32); Bst6b = Buf("st6b")
            mvb = sb2("mvb", [128, 2], F32); Bmvb = Buf("mvb")
            veb = sb2("veb", [128, 1], F32); Bveb = Buf("veb")
            rsb = sb2("rsb", [128, 1], F32); Brsb = Buf("rsb")
            Bbank2 = [Buf("bank2_%d" % i) for i in range(8)]

            def psb(i):
                return ps[:, i, :]

            wfi_v = wfi_bf.rearrange("(c p) n -> p c n", p=128)
            wfo_v = wfo_bf.rearrange("(j p) n -> p j n", p=128)
            slotB = [Buf("wslot%d" % i) for i in range(4)]
            kk = 0
            for blk in (0, 2, 3, 1, 4, 5):
                col0 = blk * 1024
                n = min(1024, 2 * D_FF - col0)
                for c in range(8):
                    S2.op("sp", (lambda e, c=c, col0=col0, n=n: e.dma_start(out=wfi_sb[:, c, col0:col0 + n],
                                                                            in_=wfi_v[:, c, col0:col0 + n])),
                          writes=[B_wfi[blk][c], slotB[kk % 4]], dma="wt%d" % (kk % 4))
                    kk += 1
            for j in range(NJ):
                S2.op("sp", (lambda e, j=j: e.dma_start(out=wfo_sb[:, j, :], in_=wfo_v[:, j, :])),
                      writes=[B_wfo[j], slotB[kk % 4]], dma="wt%d" % (kk % 4))
                kk += 1
            S2.op("sp", lambda e: e.dma_start(out=G2[:], in_=ln2_g.partition_broadcast(128)), writes=[Bc2], dma="g2")
            S2.op("sp", lambda e: e.dma_start(out=B2[:], in_=ln2_b.partition_broadcast(128)), writes=[Bc2], dma="b2")
            S2.op("dve", lambda e: e.memset(cneg2[:], -0.5), writes=[Bc2])

            rr2c = {"ab": 0}
            tiles2 = list(range(1, NT))
            groups = [tiles2[i:i + 4] for i in range(0, len(tiles2), 4)]

            def load_group(g):
                gi = g % 2
                for i, t in enumerate(groups[g]):
                    S2.op("sp", (lambda e, gi=gi, i=i, t=t: e.dma_start(
                        out=hTg[gi][:, i, :, :].rearrange("p c q -> p (c q)"), in_=h1T_d[t])),
                          writes=[BhTg[gi]], dma="hTg%d_%d" % (gi, i))

            def load_h1(n):
                fi = n % 2
                t = tiles2[n]
                S2.op("sp", lambda e: e.dma_start(out=h1in[fi][:], in_=h1_d[t * 128:(t + 1) * 128, :]),
                      writes=[Bh1in[fi]], dma="h1in%d" % fi)

            def ffn_in(g, j):
                gi = g % 2
                ntl = len(groups[g])
                ntok = ntl * 128
                ba = 2 * (rr2c["ab"] % 2)
                rr2c["ab"] += 1
                bu = ba + 1
                ai = j % 2

                def ff_in(e):
                    inst = None
                    for c in range(8):
                        inst = e.matmul(psb(ba)[:, 0:ntok], wfi_sb[:, c, j * 128:(j + 1) * 128],
                                        hTg[gi][:, 0:ntl, c, :], start=(c == 0), stop=(c == 7))
                    for c in range(8):
                        inst = e.matmul(psb(bu)[:, 0:ntok], wfi_sb[:, c, D_FF + j * 128:D_FF + (j + 1) * 128],
                                        hTg[gi][:, 0:ntl, c, :], start=(c == 0), stop=(c == 7))
                    return inst
                S2.op("pe", ff_in, reads=[BhTg[gi]] + B_wfi[j // 8] + B_wfi[(D_FF + j * 128) // 1024],
                      writes=[Bbank2[ba], Bbank2[bu]])
                S2.op("act", lambda e: e.activation(out=tha[ai][:, 0:ntok], in_=psb(ba)[:, 0:ntok], func=AF.Tanh,
                                                    scale=0.5),
                      reads=[Bbank2[ba]], writes=[Btha[ai]])
                S2.op("dve", lambda e: e.scalar_tensor_tensor(
                    out=s2b[ai][:, 0:ntok], in0=tha[ai][:, 0:ntok], scalar=1.0, in1=psb(ba)[:, 0:ntok],
                    op0=ALU.add, op1=ALU.mult),
                      reads=[Btha[ai], Bbank2[ba]], writes=[Bs2b[ai]])
                S2.op("dve", lambda e: e.tensor_tensor(
                    out=gT[:, j, 0:ntok], in0=s2b[ai][:, 0:ntok], in1=psb(bu)[:, 0:ntok], op=ALU.mult),
                      reads=[Bs2b[ai], Bbank2[bu]], writes=[BgT[j]])

            def ffn_out(n, i):
                t = tiles2[n]
                fi = n % 2
                pp = 4 + 2 * fi
                rb, Brb = rr2[fi], Brr2[fi]

                def ff_out(e):
                    inst = None
                    for nn in range(2):
                        for j in range(NJ):
                            inst = e.matmul(psb(pp + nn), gT[:, j, i * 128:(i + 1) * 128],
                                            wfo_sb[:, j, nn * 512:(nn + 1) * 512], start=(j == 0),
                                            stop=(j == NJ - 1))
                    return inst
                S2.op("pe", ff_out, reads=BgT + B_wfo, writes=[Bbank2[pp], Bbank2[pp + 1]])
                for nn in range(2):
                    S2.op("dve", (lambda e, nn=nn: e.scalar_tensor_tensor(
                        out=rb[:, nn * 512:(nn + 1) * 512], in0=psb(pp + nn), scalar=0.5 / ALPHA,
                        in1=h1in[fi][:, nn * 512:(nn + 1) * 512], op0=ALU.mult, op1=ALU.add)),
                          reads=[Bbank2[pp + nn], Bh1in[fi], Brb], writes=[Brb])

                def st_f(e):
                    e.bn_stats(out=st6b[:, 0, :], in_=rb[:, 0:512])
                    return e.bn_stats(out=st6b[:, 1, :], in_=rb[:, 512:1024])
                S2.op("dve", st_f, reads=[Brb], writes=[Bst6b])
                S2.op("dve", lambda e: e.bn_aggr(out=mvb[:], in_=st6b[:].rearrange("p a b -> p (a b)")),
                      reads=[Bst6b], writes=[Bmvb])
                S2.op("pool", lambda e: e.tensor_scalar(out=veb[:], in0=mvb[:, 1:2], scalar1=EPS / (ALPHA * ALPHA),
                                                        scalar2=None, op0=ALU.add), reads=[Bmvb], writes=[Bveb])
                S2.op("pool", lambda e: e.tensor_tensor(out=rsb[:], in0=veb[:], in1=cneg2[:, 0:1], op=ALU.pow),
                      reads=[Bveb, Bc2], writes=[Brsb])
                S2.op("dve", lambda e: e.scalar_tensor_tensor(out=rb[:], in0=rb[:], scalar=mvb[:, 0:1], in1=G2[:],
                                                              op0=ALU.subtract, op1=ALU.mult),
                      reads=[Brb, Bmvb, Bc2], writes=[Brb])
                S2.op("dve", lambda e: e.scalar_tensor_tensor(out=rb[:], in0=rb[:], scalar=rsb[:, 0:1], in1=B2[:],
                                                              op0=ALU.mult, op1=ALU.add),
                      reads=[Brb, Brsb, Bc2], writes=[Brb])
                S2.op("sp", lambda e: e.dma_start(out=out[(t - 1) * 128:t * 128, :], in_=rb[:]),
                      reads=[Brb], dma="out%d" % fi)

            if groups:
                load_group(0)
                load_h1(0)
            nflat = 0
            for g in range(len(groups)):
                if g + 1 < len(groups):
                    load_group(g + 1)
                for j in range(NJ):
                    ffn_in(g, j)
                for i in range(len(groups[g])):
                    if nflat + 1 < len(tiles2):
                        load_h1(nflat + 1)
                    ffn_out(nflat, i)
                    nflat += 1

            if skip2:
                S2.ops = []
            if limit2 is not None:
                S2.ops = S2.ops[:limit2]
            S2.finalize()
            S2.emit(nc, block, sems2, {k: dsem("p2_" + k) for k in S2.dma_counts})

            def sp_end(e):
                for k, v in S2.dma_counts.items():
                    e.wait_ge(dsem("p2_" + k), 16 * v)
            block.sync(sp_end)
    return nc


def _const_tables():
    half = 32
    inv = 10000.0 ** (-np.arange(half, dtype=np.float32) / half)
    pos = (np.arange(NTILES * 128, dtype=np.int32) - PAD).astype(np.float32)
    ang = pos[:, None] * inv[None, :]
    cos = np.cos(ang).astype(np.float32)
    sin = np.sin(ang).astype(np.float32)
    rope_t = np.concatenate([cos, cos, -sin, sin], axis=1).astype(np.float32)
    s = np.arange(128)
    same = (s[:, None] // 64) == (s[None, :] // 64)
    T64 = (same & (s[:, None] <= s[None, :])).astype(np.float32)
    U64 = (same & (s[:, None] > s[None, :])).astype(np.float32)
    ident = np.eye(128, dtype=np.float32)
    cmat = np.concatenate([T64, U64, ident], axis=1).astype(np.float32)
    return rope_t, cmat


def _in_maps(inputs, cores):
    rope_t, cmat = _const_tables()
    f = lambda a: np.ascontiguousarray(np.asarray(a, dtype=np.float32))
    shared = {
        "meta": f(inputs["meta_tokens"]),
        "ln_emb_g": f(inputs["ln_emb_g"]).reshape(1, D),
        "ln_emb_b": f(inputs["ln_emb_b"]).reshape(1, D),
        "w_in": f(inputs["w_in"])[0],
        "hg_lb": f(inputs["hg_lower_bounds"]).reshape(1, 1024),
        "hg_ng": f(inputs["hg_norm_g"]).reshape(1, 128),
        "sinks": f(inputs["attn_sinks"]).reshape(1, 8),
        "w_bhg": f(inputs["w_branch_hg"])[0],
        "w_batt": f(inputs["w_branch_attn"])[0],
        "w_out": f(inputs["w_out"])[0],
        "ln1_g": f(inputs["ln1_g"]).reshape(1, D),
        "ln1_b": f(inputs["ln1_b"]).reshape(1, D),
        "w_fi": f(inputs["w_ffn_in"])[0],
        "w_fo": f(inputs["w_ffn_out"])[0],
        "ln2_g": f(inputs["ln2_g"]).reshape(1, D),
        "ln2_b": f(inputs["ln2_b"]).reshape(1, D),
        "rope_t": rope_t,
        "cmat": cmat,
    }
    xs = np.asarray(inputs["x"], dtype=np.float32)
    return [dict(shared, x=np.ascontiguousarray(xs[b])) for b in cores]


def kernel(**inputs):
    nc = build(NTILES)
    in_maps = _in_maps(inputs, list(range(NCORES)))
    res = run_bass_kernel_spmd(nc, in_maps, core_ids=list(range(NCORES)))
    return np.stack([np.asarray(r["y_out"], dtype=np.float32) for r in res.results], axis=0)
```

```python
import math
from contextlib import ExitStack

import numpy as np
import concourse.bass as bass
import concourse.mybir as mybir
from concourse.bass_utils import run_bass_kernel_spmd

F32 = mybir.dt.float32
BF16 = mybir.dt.bfloat16
AF = mybir.ActivationFunctionType
ALU = mybir.AluOpType

D = 1024
SEQ = 8192
NTILES = SEQ // 128 + 1
N_META = 16
PAD = 112
IN_W = 4864
D_FF = 2816
EPS = 1e-5
ALPHA = 2.0 ** 0.25
NCORES = 8
DMA_SCRATCH = 4096
PREFETCH = True
PIPELINE = True
VAR = set()


class Buf:
    __slots__ = ("name", "w", "r")

    def __init__(self, name):
        self.name = name
        self.w = None
        self.r = []


class Op:
    __slots__ = ("eng", "fn", "deps", "idx", "milestone", "semval", "dma", "dmacount", "reads", "writes")

    def __init__(self, eng, fn, dma, reads, writes):
        self.eng = eng
        self.fn = fn
        self.deps = {}
        self.milestone = False
        self.semval = None
        self.dma = dma
        self.dmacount = None
        self.reads = list(reads)
        self.writes = list(writes)


class Sched:
    ENGS = ("pe", "act", "dve", "pool", "sp")

    def __init__(self):
        self.ops = []
        self.cur = None
        self.dma_counts = {}

    def op(self, eng, fn, reads=(), writes=(), dma=None):
        o = Op(eng, fn, dma, reads, writes)
        (self.cur if self.cur is not None else self.ops).append(o)
        return o

    def resolve(self):
        self.dma_counts = {}
        for o in self.ops:
            reads, writes = o.reads, o.writes
            wset = set(id(b) for b in writes)
            for b in reads:
                if b.w is not None:
                    o.deps[id(b.w)] = (b.w, "RAW")
            for b in writes:
                if b.w is not None and id(b.w) not in o.deps:
                    o.deps[id(b.w)] = (b.w, "WAW")
                for r in b.r:
                    if id(r) not in o.deps:
                        o.deps[id(r)] = (r, "WAR")
            for b in writes:
                b.w = o
                b.r = []
            for b in reads:
                if id(b) not in wset:
                    b.r.append(o)
            if o.dma is not None:
                self.dma_counts[o.dma] = self.dma_counts.get(o.dma, 0) + 1
                o.dmacount = self.dma_counts[o.dma]

    @staticmethod
    def _needed(o, d, kind):
        if d.dma is not None:
            return True
        if d.eng == o.eng and kind != "RAW":
            return False
        return True

    def finalize(self):
        self.resolve()
        for o in self.ops:
            for d, kind in o.deps.values():
                if d.dma is None and self._needed(o, d, kind):
                    d.milestone = True
        last = {}
        for o in self.ops:
            if o.dma is None:
                last[o.eng] = o
        for o in last.values():
            o.milestone = True
        cnt = {e: 0 for e in self.ENGS}
        for o in self.ops:
            if o.dma is None and o.milestone:
                cnt[o.eng] += 1
                o.semval = cnt[o.eng]

    def emit(self, nc, block, sems, dma_sems):
        engobj = {"pe": "tensor", "act": "scalar", "dve": "vector", "pool": "gpsimd", "sp": "sync"}
        for eng in self.ENGS:
            myops = [o for o in self.ops if o.eng == eng]
            if not myops:
                continue

            def body(e, myops=myops, eng=eng):
                waited = {}
                for o in myops:
                    need = {}
                    for d, kind in o.deps.values():
                        if not self._needed(o, d, kind):
                            continue
                        if d.dma is not None:
                            key = ("d", d.dma)
                            val = 16 * d.dmacount
                        else:
                            key = ("e", d.eng)
                            val = d.semval
                        if val > need.get(key, 0):
                            need[key] = val
                    for key, val in need.items():
                        if waited.get(key, 0) >= val:
                            continue
                        waited[key] = val
                        sem = dma_sems[key[1]] if key[0] == "d" else sems[key[1]]
                        e.wait_ge(sem, val)
                    inst = o.fn(e)
                    if o.dma is not None:
                        inst.then_inc(dma_sems[o.dma], 16)
                    elif o.milestone:
                        inst.then_inc(sems[eng], 1)

            getattr(block, engobj[eng])(body)


def build(NT=NTILES, dbg=False, limit1=None, skip2=False, limit2=None):
    nc = bass.Bass("TRN2", target_bir_lowering=False, dynamic_dma_scratch_size=DMA_SCRATCH)
    NG2 = (NT - 1 + 3) // 4
    NOUT = (NT - 1) * 128

    def din(name, shape, dt=F32):
        return nc.dram_tensor(name, list(shape), dt, kind="ExternalInput").ap()

    x = din("x", [SEQ, D])
    meta = din("meta", [N_META, D])
    ln_emb_g = din("ln_emb_g", [1, D])
    ln_emb_b = din("ln_emb_b", [1, D])
    w_in = din("w_in", [D, IN_W])
    hg_lb = din("hg_lb", [1, 1024])
    hg_ng = din("hg_ng", [1, 128])
    sinks = din("sinks", [1, 8])
    w_bhg = din("w_bhg", [512, D])
    w_batt = din("w_batt", [512, D])
    w_out = din("w_out", [D, D])
    ln1_g = din("ln1_g", [1, D])
    ln1_b = din("ln1_b", [1, D])
    w_fi = din("w_fi", [D, 2 * D_FF])
    w_fo = din("w_fo", [D_FF, D])
    ln2_g = din("ln2_g", [1, D])
    ln2_b = din("ln2_b", [1, D])
    rope_t = din("rope_t", [NTILES * 128, 128])
    cmat = din("cmat", [128, 3 * 128])
    out = nc.dram_tensor("y_out", [NOUT, D], F32, kind="ExternalOutput").ap()
    h1_d = nc.dram_tensor("h1_scr", [NTILES * 128, D], F32, kind="Internal").ap()
    h1T_d = nc.dram_tensor("h1T_scr", [NTILES, 128, 1024], BF16, kind="Internal").ap()
    wfi_bf = nc.dram_tensor("wfi_bf", [D, 2 * D_FF], BF16, kind="Internal").ap()
    wfo_bf = nc.dram_tensor("wfo_bf", [D_FF, D], BF16, kind="Internal").ap()

    es = ExitStack()
    with es:
        def sb(name, shape, dt):
            return es.enter_context(nc.sbuf_tensor(name, list(shape), dt))

        sem_names = list(Sched.ENGS)
        sems = {e: es.enter_context(nc.semaphore("s_" + e)) for e in sem_names}
        sems2 = {e: es.enter_context(nc.semaphore("t_" + e)) for e in sem_names}
        fence_sem = es.enter_context(nc.semaphore("fence"))
        dma_sem_store = {}

        def dsem(name):
            if name not in dma_sem_store:
                dma_sem_store[name] = es.enter_context(nc.semaphore("d_" + name))
            return dma_sem_store[name]

        ps = es.enter_context(nc.psum_tensor("ps", [128, 8, 512], F32))
        block = es.enter_context(nc.Block())

        S = Sched()
        p1 = ExitStack()
        with p1:
            def sb1(name, shape, dt):
                return p1.enter_context(nc.sbuf_tensor(name, list(shape), dt))

            win_sb = sb1("win", [128, 8, IN_W], BF16)
            wbh_sb = sb1("wbh", [128, 4, D], BF16)
            wba_sb = sb1("wba", [128, 4, D], BF16)
            wo_sb = sb1("wo", [128, 8, D], BF16)
            B_win = [Buf("win%d" % c) for c in range(5)]
            B_wbh, B_wba, B_wo = Buf("wbh"), Buf("wba"), Buf("wo")
            Gemb = sb1("Gemb", [128, D], F32); Bemb = sb1("Bemb", [128, D], F32)
            G1 = sb1("G1", [128, D], F32); B1 = sb1("B1", [128, D], F32)
            C2 = sb1("C2", [128, 512], F32)
            ngcol = sb1("ngcol", [128, 1], F32)
            cm = sb1("cm", [128, 3, 128], F32)
            mask64 = sb1("mask64", [128, 128], BF16)
            ident = sb1("ident", [128, 128], BF16)
            esink = sb1("esink", [128, 8], F32)
            cneg = sb1("cneg", [128, 8], F32)
            vmask = sb1("vmask", [128, 1], F32)
            B_const = Buf("const")
            def dbl(name, shape, dt, n=2):
                return ([sb1("%s_%d" % (name, i), shape, dt) for i in range(n)],
                        [Buf("%s_%d" % (name, i)) for i in range(n)])

            x32, Bx32 = dbl("x32", [128, D], F32)
            hA, BhA = dbl("hA", [128, D], F32, 3)
            rt, Brt = dbl("rt", [128, 128], F32)
            stat = []
            for i in range(2):
                stat.append(dict(st6=sb1("st6_%d" % i, [128, 2, 6], F32), Bst6=Buf("st6"),
                                 mv=sb1("mv_%d" % i, [128, 2], F32), Bmv=Buf("mv"),
                                 ve=sb1("ve_%d" % i, [128, 1], F32), Bve=Buf("ve"),
                                 rs=sb1("rs_%d" % i, [128, 1], F32), Brs=Buf("rs")))
            hbf = sb1("hbf", [128, D], BF16); Bhbf = Buf("hbf")
            hT, BhT = dbl("hT", [128, 8, 128], BF16, 3)
            q2 = sb1("q2", [128, 512], F32); Bq2 = Buf("q2")
            k32 = sb1("k32", [128, 512], F32); Bk32 = Buf("k32")
            lnf = sb1("lnf", [128, 512], F32); Blnf = Buf("lnf")
            eb = [sb1("eb%d" % i, [128, 512], F32) for i in range(3)]
            Beb = [Buf("eb%d" % i) for i in range(3)]
            qt = sb1("qt", [128, 512], BF16); Bqt = Buf("qt")
            kt = sb1("kt", [128, 512], BF16); Bkt = Buf("kt")
            thq, Bthq, thf, Bthf = qt, Bqt, kt, Bkt
            kb, Bkb = dbl("kb", [128, 512], BF16)
            qkT, BqkT = dbl("qkT", [128, 8, 128], BF16)
            A_sb = sb1("A_sb", [128, 4, 128], BF16); BA_sb = Buf("A_sb")
            v_bf, Bv_bf = dbl("v_bf", [128, 512], BF16)
            gn2, Bgn2 = dbl("gn2", [128, 512], F32)
            ogs, Bogs = dbl("og", [128, 512], BF16)
            oT = sb1("oT", [128, 4, 128], BF16); BoT = Buf("oT")
            S32 = sb1("S32", [128, 4, 128], F32); BS32 = Buf("S32")
            Sbf = sb1("Sbf", [128, 4, 128], BF16); BSbf = Buf("Sbf")
            dec, Bdec = dbl("dec", [128, 8], F32)
            ss = sb1("ss", [128, 4], F32); Bss = Buf("ss")
            rsn = sb1("rsn", [128, 4], F32); Brsn = Buf("rsn")
            qr = sb1("qr", [128, 512], BF16); Bqr = Buf("qr")
            thg, Bthg = qr, Bqr
            kr = sb1("kr", [128, 128], BF16); Bkr = Buf("kr")
            qT, BqT = dbl("qT", [128, 4, 128], BF16)
            kTb, BkT = dbl("kT", [128, 128], BF16, 3)
            kTm = sb1("kTm", [128, 16], BF16); BkTm = Buf("kTm")
            vaug, Bvaug = dbl("vaug", [128, 2, 65], BF16, 3)
            vmeta = sb1("vmeta", [16, 2, 65], BF16); Bvmeta = Buf("vmeta")
            Ecur = sb1("Ecur", [128, 2, 512], BF16); BEcur = Buf("Ecur")
            Eprev = sb1("Eprev", [128, 2, 512], BF16); BEprev = Buf("Eprev")
            Emeta = sb1("Emeta", [16, 2, 512], BF16); BEmeta = Buf("Emeta")
            den = sb1("den", [128, 8], F32); Bden = Buf("den")
            rden = sb1("rden", [128, 8], F32); Brden = Buf("rden")
            aos, Baos = dbl("ao", [128, 512], BF16)
            thG = sb1("thG", [128, 2048], BF16); BthG = Buf("thG")
            m1 = sb1("m1", [128, D], BF16); Bm1 = Buf("m1")
            mm = sb1("mm", [128, D], BF16); Bmm = Buf("mm")
            mT = sb1("mT", [128, 8, 128], BF16); BmT = Buf("mT")
            r32 = sb1("r32", [128, D], F32); Br32 = Buf("r32")
            h1T = sb1("h1T", [128, 8, 128], BF16); Bh1T = Buf("h1T")

            Bbank = [Buf("bank%d" % i) for i in range(8)]
            rr = {"a": 0, "b": 0, "p": 0}

            def bankA():
                i = rr["a"]
                rr["a"] = (i + 1) % 3
                return i

            B1SEQ = [3, 4, 5, 4]
            B1CYC = [3, 5, 4]

            def bankB1():
                i = rr["b"]
                rr["b"] = i + 1
                return B1SEQ[i] if i < 4 else B1CYC[(i - 4) % 3]

            def bankB2():
                i = rr["p"]
                rr["p"] = (i + 1) % 2
                return 6 + i

            def pair():
                return 6

            def psb(i):
                return ps[:, i, :]

            def psb16(i):
                return ps[:, i, :].bitcast(BF16)

            setup_ops = []
            S.cur = setup_ops

            def bc_load(dst, src_row, n):
                return lambda e: e.dma_start(out=dst, in_=src_row.partition_broadcast(128))

            S.op("sp", lambda e: e.dma_start(out=cm[:].rearrange("p a b -> p (a b)"), in_=cmat),
                 writes=[B_const], dma="c0")
            S.op("sp", bc_load(Gemb[:], ln_emb_g, D), writes=[B_const], dma="c1")
            S.op("sp", bc_load(Bemb[:], ln_emb_b, D), writes=[B_const], dma="c2")
            S.op("sp", bc_load(G1[:], ln1_g, D), writes=[B_const], dma="c3")
            S.op("sp", bc_load(B1[:], ln1_b, D), writes=[B_const], dma="c4")
            S.op("sp", bc_load(eb[0][:], hg_lb[:, 0:512], 512), writes=[Beb[0]], dma="c5a")
            S.op("sp", bc_load(eb[1][:], hg_lb[:, 512:1024], 512), writes=[Beb[1]], dma="c5b")
            S.op("sp", bc_load(esink[:], sinks, 8), writes=[B_const], dma="c6")
            S.op("sp", lambda e: e.dma_start(out=ngcol[:], in_=hg_ng.rearrange("a p -> p a")), writes=[B_const],
                 dma="c7")
            S.op("dve", lambda e: e.tensor_copy(out=mask64[:], in_=cm[:, 0, :]), reads=[B_const], writes=[B_const])
            S.op("dve", lambda e: e.tensor_copy(out=ident[:], in_=cm[:, 2, :]), reads=[B_const], writes=[B_const])
            S.op("dve", lambda e: e.memset(cneg[:], -0.5), writes=[B_const])
            S.op("dve", lambda e: e.memset(vmask[:], 1.0), writes=[B_const])
            S.op("pool", lambda e: e.affine_select(out=vmask[:], in_=vmask[:], pattern=[[0, 1]],
                                                   compare_op=ALU.is_ge, fill=0.0, base=-PAD,
                                                   channel_multiplier=1),
                 reads=[B_const], writes=[B_const])
            S.op("dve", lambda e: e.tensor_tensor(out=C2[:], in0=eb[0][:], in1=eb[1][:], op=ALU.subtract),
                 reads=[Beb[0], Beb[1]], writes=[B_const])
            S.op("act", lambda e: e.activation(out=C2[:], in_=C2[:], func=AF.Tanh, scale=0.5),
                 reads=[B_const], writes=[B_const])
            S.op("dve", lambda e: e.tensor_scalar(out=C2[:], in0=C2[:], scalar1=-0.25, scalar2=0.25,
                                                  op0=ALU.mult, op1=ALU.add),
                 reads=[B_const], writes=[B_const])
            S.op("act", lambda e: e.activation(out=esink[:], in_=esink[:], func=AF.Exp),
                 reads=[B_const], writes=[B_const])
            S.op("dve", lambda e: e.memset(S32[:], 0.0), writes=[BS32])
            S.op("dve", lambda e: e.memset(Sbf[:], 0.0), writes=[BSbf])
            for i in range(3):
                S.op("dve", (lambda e, i=i: e.memset(vaug[i][:], 1.0)), writes=[Bvaug[i]])

            stage1 = [(x32[0], Bx32[0]), (x32[1], Bx32[1]), (hA[0], BhA[0]), (hA[1], BhA[1]), (r32, Br32)]
            wl = {"i": 0}

            def load_w(Sx, stages, dst_ap, src_ap, n, Bdst, tag, scale_col=None):
                k = wl["i"]
                wl["i"] += 1
                stg, Bstg = stages[k % len(stages)]
                ce = ("act", "dve")[k % 2]
                Sx.op("sp", lambda e: e.dma_start(out=stg[:, 0:n], in_=src_ap), writes=[Bstg],
                      dma="%s%d" % (tag, k % len(stages)))
                if scale_col is not None:
                    Sx.op("dve", lambda e: e.tensor_scalar(out=dst_ap, in0=stg[:, 0:n], scalar1=scale_col[:, 0:1],
                                                           scalar2=0.5, op0=ALU.mult, op1=ALU.mult),
                          reads=[Bstg, B_const], writes=[Bdst])
                elif ce == "act":
                    Sx.op("act", lambda e: e.copy(out=dst_ap, in_=stg[:, 0:n]), reads=[Bstg], writes=[Bdst])
                else:
                    Sx.op(ce, lambda e: e.tensor_copy(out=dst_ap, in_=stg[:, 0:n]), reads=[Bstg], writes=[Bdst])

            w_in_v = w_in.rearrange("(c p) n -> p c n", p=128)
            for col0 in range(0, IN_W, 1024):
                for c in range(8):
                    n = min(1024, IN_W - col0)
                    load_w(S, stage1, win_sb[:, c, col0:col0 + n], w_in_v[:, c, col0:col0 + n], n,
                           B_win[col0 // 1024], "ws")
            wbh_v = w_bhg.rearrange("(c p) n -> p c n", p=128)
            wba_v = w_batt.rearrange("(c p) n -> p c n", p=128)
            wo_v = w_out.rearrange("(c p) n -> p c n", p=128)
            for c in range(4):
                load_w(S, stage1, wbh_sb[:, c, :], wbh_v[:, c, :], 1024, B_wbh, "ws", scale_col=ngcol)
            for c in range(4):
                load_w(S, stage1, wba_sb[:, c, :], wba_v[:, c, :], 1024, B_wba, "ws")
            for c in range(8):
                load_w(S, stage1, wo_sb[:, c, :], wo_v[:, c, :], 1024, B_wo, "ws")

            cvB = [Buf("cv%d" % i) for i in range(2)]
            ncv = 0
            cv_ops = []
            S.cur = cv_ops
            for r0 in range(0, D, 128):
                S.op("pool", (lambda e, r0=r0: e.dma_start(out=wfi_bf[r0:r0 + 128, :], in_=w_fi[r0:r0 + 128, :])),
                     reads=[B_wo], writes=[cvB[ncv % 2]], dma="cv%d" % (ncv % 2))
                ncv += 1
            for r0 in range(0, D_FF, 128):
                S.op("pool", (lambda e, r0=r0: e.dma_start(out=wfo_bf[r0:r0 + 128, :], in_=w_fo[r0:r0 + 128, :])),
                     writes=[cvB[ncv % 2]], dma="cv%d" % (ncv % 2))
                ncv += 1

            S.cur = setup_ops
            T64 = cm[:, 0, :]
            U64 = cm[:, 1, :]

            def transposes(src_fn, n, dst2, BdstT, Bsrc, evac_eng, bk):
                pv = psb16(bk)

                def pe_fn(e):
                    inst = None
                    for i in range(n):
                        inst = e.transpose(pv[:, i * 128:(i + 1) * 128], src_fn(i), ident[:])
                    return inst
                S.op("pe", pe_fn, reads=[Bsrc, B_const], writes=[Bbank[bk]])
                if evac_eng == "act":
                    S.op("act", lambda e: e.copy(out=dst2, in_=pv[:, 0:n * 128]), reads=[Bbank[bk]], writes=[BdstT])
                else:
                    S.op("dve", lambda e: e.tensor_copy(out=dst2, in_=pv[:, 0:n * 128]), reads=[Bbank[bk]],
                         writes=[BdstT])

            def flat3(tl):
                return tl[:].rearrange("p a b -> p (a b)")

            def proj(hTt, BhTt, col0, ncols, bk):
                def pe_fn(e):
                    inst = None
                    for c in range(8):
                        inst = e.matmul(psb(bk)[:, 0:ncols], hTt[:, c, :], win_sb[:, c, col0:col0 + ncols],
                                        start=(c == 0), stop=(c == 7))
                    return inst
                S.op("pe", pe_fn, reads=[BhTt] + B_win[col0 // 1024:(col0 + ncols - 1) // 1024 + 1],
                     writes=[Bbank[bk]])
                return bk

            def ln_stats(sd, src, Bsrc, dst, Bdst, Gt, Bt, eps):
                st6, mv, ve, rs = sd["st6"], sd["mv"], sd["ve"], sd["rs"]

                def f(e):
                    e.bn_stats(out=st6[:, 0, :], in_=src[:, 0:512])
                    return e.bn_stats(out=st6[:, 1, :], in_=src[:, 512:1024])
                S.op("dve", f, reads=[Bsrc], writes=[sd["Bst6"]])
                S.op("dve", lambda e: e.bn_aggr(out=mv[:], in_=st6[:].rearrange("p a b -> p (a b)")),
                     reads=[sd["Bst6"]], writes=[sd["Bmv"]])
                S.op("pool", lambda e: e.tensor_scalar(out=ve[:], in0=mv[:, 1:2], scalar1=eps, scalar2=None,
                                                       op0=ALU.add), reads=[sd["Bmv"]], writes=[sd["Bve"]])
                S.op("pool", lambda e: e.tensor_tensor(out=rs[:], in0=ve[:], in1=cneg[:, 0:1], op=ALU.pow),
                     reads=[sd["Bve"], B_const], writes=[sd["Brs"]])
                S.op("dve", lambda e: e.scalar_tensor_tensor(out=src[:], in0=src[:], scalar=mv[:, 0:1],
                                                             in1=Gt[:], op0=ALU.subtract, op1=ALU.mult),
                     reads=[Bsrc, sd["Bmv"], B_const], writes=[Bsrc])
                S.op("dve", lambda e: e.scalar_tensor_tensor(out=dst[:], in0=src[:], scalar=rs[:, 0:1],
                                                             in1=Bt[:], op0=ALU.mult, op1=ALU.add),
                     reads=[Bsrc, sd["Brs"], B_const] + ([Bdst] if Bdst is not Bsrc else []), writes=[Bdst])

            def load_tile(t):
                i2 = t % 2
                xb, Bxb = x32[i2], Bx32[i2]
                if t == 0:
                    S.op("pool", lambda e: e.memset(xb[:], 0.0), writes=[Bxb])
                    S.op("sp", lambda e: e.dma_start(out=xb[PAD:128, :], in_=meta), reads=[Bxb], writes=[Bxb],
                         dma="x0")
                else:
                    S.op("sp", lambda e: e.dma_start(out=xb[:], in_=x[(t - 1) * 128:t * 128, :]),
                         writes=[Bxb], dma="x%d" % i2)
                S.op("sp", lambda e: e.dma_start(out=rt[i2][:], in_=rope_t[t * 128:(t + 1) * 128, :]),
                     writes=[Brt[i2]], dma="rt%d" % i2)

            def do_A(t):
                i2, i3 = t % 2, t % 3
                xb, Bxb = x32[i2], Bx32[i2]
                hTt, BhTt = hT[i3], BhT[i3]
                if t + 1 < NT:
                    load_tile(t + 1)
                ln_stats(stat[0], xb, Bxb, hA[i3], BhA[i3], Gemb, Bemb, EPS)
                if t == 0:
                    S.op("dve", lambda e: e.tensor_scalar(out=hA[i3][:], in0=hA[i3][:], scalar1=vmask[:, 0:1],
                                                          scalar2=None, op0=ALU.mult),
                         reads=[BhA[i3], B_const], writes=[BhA[i3]])
                S.op("act", lambda e: e.copy(out=hbf[:], in_=hA[i3][:]), reads=[BhA[i3]], writes=[Bhbf])
                transposes(lambda i: hbf[:, i * 128:(i + 1) * 128], 8, flat3(hTt), BhTt, Bhbf, "act", bankA())

                bq = proj(hTt, BhTt, 0, 512, bankA())
                S.op("act", lambda e: e.activation(out=thq[:], in_=psb(bq), func=AF.Tanh, scale=0.5),
                     reads=[Bbank[bq]], writes=[Bthq])
                S.op("dve", lambda e: e.scalar_tensor_tensor(out=q2[:], in0=thq[:], scalar=1.0, in1=psb(bq),
                                                             op0=ALU.add, op1=ALU.mult),
                     reads=[Bthq, Bbank[bq]], writes=[Bq2])
                bf_ = proj(hTt, BhTt, 512, 512, bankA())
                S.op("act", lambda e: e.activation(out=thf[:], in_=psb(bf_), func=AF.Tanh, scale=-0.5),
                     reads=[Bbank[bf_]], writes=[Bthf])
                S.op("dve", lambda e: e.scalar_tensor_tensor(out=k32[:], in0=thf[:], scalar=1.0, in1=C2[:],
                                                             op0=ALU.add, op1=ALU.mult),
                     reads=[Bthf, B_const], writes=[Bk32])
                if t == 0:
                    S.op("dve", lambda e: e.tensor_scalar(out=k32[:], in0=k32[:], scalar1=vmask[:, 0:1], scalar2=None,
                                                          op0=ALU.mult), reads=[Bk32, B_const], writes=[Bk32])
                bv = proj(hTt, BhTt, 1024, 512, bankA())
                S.op("act", lambda e: e.copy(out=v_bf[i2][:], in_=psb(bv)), reads=[Bbank[bv]], writes=[Bv_bf[i2]])
                bg = proj(hTt, BhTt, 1536, 512, bankA())
                S.op("act", lambda e: e.activation(out=thg[:], in_=psb(bg), func=AF.Tanh, scale=0.5),
                     reads=[Bbank[bg]], writes=[Bthg])
                S.op("dve", lambda e: e.scalar_tensor_tensor(out=gn2[i2][:], in0=thg[:], scalar=1.0, in1=psb(bg),
                                                             op0=ALU.add, op1=ALU.mult),
                     reads=[Bthg, Bbank[bg]], writes=[Bgn2[i2]])

                S.op("act", lambda e: e.activation(out=lnf[:], in_=k32[:], func=AF.Ln, scale=-1.0, bias=1.0),
                     reads=[Bk32], writes=[Blnf])
                bP = bankA()
                S.op("pe", lambda e: e.matmul(psb(bP), T64, lnf[:], start=True, stop=True),
                     reads=[Blnf, B_const], writes=[Bbank[bP]])
                bS = bankA()
                S.op("pe", lambda e: e.matmul(psb(bS), U64, lnf[:], start=True, stop=True),
                     reads=[Blnf, B_const], writes=[Bbank[bS]])
                bD = bankA()

                def dec_fn(e):
                    inst = None
                    for hh in range(4):
                        inst = e.matmul(psb(bD)[:, 2 * hh:2 * hh + 2], lnf[:, hh * 128:(hh + 1) * 128],
                                        cm[:, 0, 63:128:64], start=True, stop=True)
                    return inst
                S.op("pe", dec_fn, reads=[Blnf, B_const], writes=[Bbank[bD]])
                S.op("act", lambda e: e.activation(out=eb[0][:], in_=psb(bP), func=AF.Exp, bias=math.log(0.5)),
                     reads=[Bbank[bP]], writes=[Beb[0]])
                S.op("dve", lambda e: e.tensor_tensor(out=qt[:], in0=q2[:], in1=eb[0][:], op=ALU.mult),
                     reads=[Bq2, Beb[0]], writes=[Bqt])
                S.op("act", lambda e: e.activation(out=eb[1][:], in_=psb(bP), func=AF.Exp, scale=-1.0),
                     reads=[Bbank[bP]], writes=[Beb[1]])
                S.op("dve", lambda e: e.tensor_tensor(out=kt[:], in0=k32[:], in1=eb[1][:], op=ALU.mult),
                     reads=[Bk32, Beb[1]], writes=[Bkt])
                S.op("act", lambda e: e.activation(out=eb[2][:], in_=psb(bS), func=AF.Exp),
                     reads=[Bbank[bS]], writes=[Beb[2]])
                S.op("dve", lambda e: e.tensor_tensor(out=kb[i2][:], in0=k32[:], in1=eb[2][:], op=ALU.mult),
                     reads=[Bk32, Beb[2]], writes=[Bkb[i2]])
                S.op("act", lambda e: e.activation(out=dec[i2][:], in_=psb(bD)[:, 0:8], func=AF.Exp),
                     reads=[Bbank[bD]], writes=[Bdec[i2]])

                bqk = bankA()
                pvq = psb16(bqk)

                def qk_tr(e):
                    inst = None
                    for hh in range(4):
                        inst = e.transpose(pvq[:, hh * 128:(hh + 1) * 128], qt[:, hh * 128:(hh + 1) * 128], ident[:])
                    for hh in range(4):
                        inst = e.transpose(pvq[:, (4 + hh) * 128:(5 + hh) * 128], kt[:, hh * 128:(hh + 1) * 128],
                                           ident[:])
                    return inst
                S.op("pe", qk_tr, reads=[Bqt, Bkt, B_const], writes=[Bbank[bqk]])
                S.op("dve", lambda e: e.tensor_copy(out=flat3(qkT[i2]), in_=pvq), reads=[Bbank[bqk]],
                     writes=[BqkT[i2]])

                R1 = rt[i2][:, 0:64]
                R2a = rt[i2][:, 64:96]
                R2b = rt[i2][:, 96:128]

                def dst_in(tb, nh):
                    if nh == 8:
                        return tb[:].rearrange("p (a j d) -> p a j d", a=2, j=4)
                    return tb[:, 0:128]

                def rope(bk_, nh, dst_ap, Bdst, t1, Bt1, t2, Bt2):
                    src = psb(bk_)[:, 0:nh * 64].rearrange("p (h d) -> p h d", d=64)
                    t1v = t1[:, 0:nh * 64].rearrange("p (h d) -> p h d", d=64)
                    t2v = t2[:, 0:nh * 64].rearrange("p (h d) -> p h d", d=64)
                    S.op("dve", lambda e: e.tensor_tensor(out=t1v, in0=src,
                                                          in1=R1.unsqueeze(1).to_broadcast([128, nh, 64]),
                                                          op=ALU.mult),
                         reads=[Bbank[bk_], Brt[i2]], writes=[Bt1])

                    def f2(e):
                        e.tensor_tensor(out=t2v[:, :, 0:32], in0=src[:, :, 32:64],
                                        in1=R2a.unsqueeze(1).to_broadcast([128, nh, 32]), op=ALU.mult)
                        return e.tensor_tensor(out=t2v[:, :, 32:64], in0=src[:, :, 0:32],
                                               in1=R2b.unsqueeze(1).to_broadcast([128, nh, 32]), op=ALU.mult)
                    S.op("dve", f2, reads=[Bbank[bk_], Brt[i2]], writes=[Bt2])
                    S.op("dve", lambda e: e.tensor_tensor(out=dst_ap, in0=dst_in(t1, nh), in1=dst_in(t2, nh),
                                                          op=ALU.add),
                         reads=[Bt1, Bt2], writes=[Bdst])

                bakv = proj(hTt, BhTt, 2560, 256, bankA())
                rope(bakv, 2, kr[:], Bkr, eb[0], Beb[0], eb[1], Beb[1])
                S.op("dve", lambda e: e.tensor_copy(
                    out=vaug[i3][:, :, 0:64], in_=psb(bakv)[:, 128:256].rearrange("p (g d) -> p g d", d=64)),
                     reads=[Bbank[bakv]], writes=[Bvaug[i3]])
                bkT = bankA()
                pvk = psb16(bkT)
                S.op("pe", lambda e: e.transpose(pvk[:, 0:128], kr[:], ident[:]),
                     reads=[Bkr, B_const], writes=[Bbank[bkT]])
                S.op("dve", lambda e: e.tensor_copy(out=kTb[i3][:], in_=pvk[:, 0:128]),
                     reads=[Bbank[bkT]], writes=[BkT[i3]])
                if t == 0:
                    S.op("dve", lambda e: e.tensor_copy(out=kTm[:], in_=kTb[0][:, PAD:128]), reads=[BkT[0]],
                         writes=[BkTm])
                    S.op("sp", lambda e: e.dma_start(out=vmeta[:], in_=vaug[0][PAD:128, :, :]), reads=[Bvaug[0]],
                         writes=[Bvmeta], dma="vmeta")
                    return
                baq = proj(hTt, BhTt, 2048, 512, bankA())
                rope(baq, 8, qr[:].rearrange("p (j a d) -> p a j d", a=2, d=64), Bqr, eb[2], Beb[2], eb[0], Beb[0])
                transposes(lambda i: qr[:, i * 128:(i + 1) * 128], 4, flat3(qT[i2]), BqT[i2], Bqr, "dve", bankA())

            def do_B1(t):
                i2, i3, p3 = t % 2, t % 3, (t - 1) % 3
                rr["b"] = 0
                bankB = bankB1
                og, Bog, ao, Bao = ogs[i2], Bogs[i2], aos[i2], Baos[i2]
                qk, Bqk = qkT[i2], BqkT[i2]
                vb, Bvb = v_bf[i2], Bv_bf[i2]
                kbt, Bkbt = kb[i2], Bkb[i2]
                dct, Bdct = dec[i2], Bdec[i2]
                bO = bankB()
                if t > 0:
                    bA = bankB()

                    def a_fn(e):
                        inst = None
                        for hh in range(4):
                            inst = e.matmul(psb(bA)[:, hh * 128:(hh + 1) * 128], qk[:, 4 + hh, :], qk[:, hh, :],
                                            start=True, stop=True)
                        return inst
                    S.op("pe", a_fn, reads=[Bqk], writes=[Bbank[bA]])
                    S.op("dve", lambda e: e.tensor_tensor(
                        out=A_sb[:], in0=psb(bA).rearrange("p (h t) -> p h t", h=4),
                        in1=mask64[:].unsqueeze(1).to_broadcast([128, 4, 128]), op=ALU.mult),
                         reads=[Bbank[bA], B_const], writes=[BA_sb])

                    def o1_fn(e):
                        inst = None
                        for hh in range(4):
                            inst = e.matmul(psb(bO)[:, hh * 128:(hh + 1) * 128], A_sb[:, hh, :],
                                            vb[:, hh * 128:(hh + 1) * 128], start=(hh == 0), stop=False,
                                            skip_group_check=True)
                        for hh in range(4):
                            inst = e.matmul(psb(bO)[0:64, hh * 128:(hh + 1) * 128], qk[:, hh, 0:64], Sbf[:, hh, :],
                                            start=False, stop=False, skip_group_check=True)
                        return inst
                    S.op("pe", o1_fn, reads=[BA_sb, Bvb, Bqk, BSbf], writes=[Bbank[bO]])
                bSa = bankB()
                bSb = bankB()

                def st_fn(e):
                    inst = None
                    for hh in range(4):
                        inst = e.matmul(psb(bSa)[:, hh * 128:(hh + 1) * 128], kbt[0:64, hh * 128:(hh + 1) * 128],
                                        vb[0:64, hh * 128:(hh + 1) * 128], start=True, stop=True)
                    for hh in range(4):
                        inst = e.matmul(psb(bSb)[:, hh * 128:(hh + 1) * 128], kbt[64:128, hh * 128:(hh + 1) * 128],
                                        vb[64:128, hh * 128:(hh + 1) * 128], start=True, stop=True)
                    return inst
                S.op("pe", st_fn, reads=[Bkbt, Bvb], writes=[Bbank[bSa], Bbank[bSb]])

                def upd(e, b, col):
                    inst = None
                    for hh in range(4):
                        inst = e.scalar_tensor_tensor(out=S32[:, hh, :], in0=S32[:, hh, :],
                                                      scalar=dct[:, 2 * hh + col:2 * hh + col + 1],
                                                      in1=psb(b)[:, hh * 128:(hh + 1) * 128],
                                                      op0=ALU.mult, op1=ALU.add)
                    return inst
                S.op("dve", lambda e: upd(e, bSa, 0), reads=[BS32, Bdct, Bbank[bSa], BSbf], writes=[BS32])
                S.op("act", lambda e: e.copy(out=flat3(Sbf), in_=flat3(S32)), reads=[BS32], writes=[BSbf])
                if t > 0:
                    def o2_fn(e):
                        inst = None
                        for hh in range(4):
                            inst = e.matmul(psb(bO)[64:128, hh * 128:(hh + 1) * 128], qk[:, hh, 64:128], Sbf[:, hh, :],
                                            start=False, stop=(hh == 3), skip_group_check=True)
                        return inst
                    S.op("pe", o2_fn, reads=[Bqk, BSbf, Bbank[bO]], writes=[Bbank[bO]])
                S.op("dve", lambda e: upd(e, bSb, 1), reads=[BS32, Bdct, Bbank[bSb], BSbf], writes=[BS32])
                S.op("act", lambda e: e.copy(out=flat3(Sbf), in_=flat3(S32)), reads=[BS32], writes=[BSbf])
                if t == 0:
                    return

                def sq_fn(e):
                    inst = None
                    for hh in range(4):
                        inst = e.activation(out=og[:, hh * 128:(hh + 1) * 128], in_=psb(bO)[:, hh * 128:(hh + 1) * 128],
                                            func=AF.Square, accum_out=ss[:, hh:hh + 1])
                    return inst
                S.op("act", sq_fn, reads=[Bbank[bO]], writes=[Bss, Bog])
                S.op("pool", lambda e: e.tensor_scalar(out=rsn[:], in0=ss[:], scalar1=1.0 / 128, scalar2=EPS,
                                                       op0=ALU.mult, op1=ALU.add), reads=[Bss], writes=[Brsn])
                S.op("pool", lambda e: e.tensor_tensor(out=rsn[:], in0=rsn[:], in1=cneg[:, 0:4], op=ALU.pow),
                     reads=[Brsn, B_const], writes=[Brsn])

                def og_fn(e):
                    inst = None
                    for hh in range(4):
                        inst = e.scalar_tensor_tensor(out=og[:, hh * 128:(hh + 1) * 128],
                                                      in0=psb(bO)[:, hh * 128:(hh + 1) * 128],
                                                      scalar=rsn[:, hh:hh + 1],
                                                      in1=gn2[i2][:, hh * 128:(hh + 1) * 128],
                                                      op0=ALU.mult, op1=ALU.mult)
                    return inst
                S.op("dve", og_fn, reads=[Bbank[bO], Brsn, Bgn2[i2], Bog], writes=[Bog])

                qT2 = flat3(qT[i2])

                def scores(kT_ap, nkeys, E, BE, Bk, mask):
                    bs0, bs1 = bankB(), bankB()

                    def f(e):
                        e.matmul(psb(bs0)[0:nkeys, :], kT_ap[0:64, 0:nkeys], qT2[0:64, :], start=True, stop=True)
                        return e.matmul(psb(bs1)[0:nkeys, :], kT_ap[64:128, 0:nkeys], qT2[64:128, :], start=True,
                                        stop=True)
                    S.op("pe", f, reads=[Bk, BqT[i2]], writes=[Bbank[bs0], Bbank[bs1]])
                    S.op("act", lambda e: e.activation(out=E[0:nkeys, 0, :], in_=psb(bs0)[0:nkeys, :], func=AF.Exp,
                                                       scale=0.125), reads=[Bbank[bs0]], writes=[BE])
                    S.op("act", lambda e: e.activation(out=E[0:nkeys, 1, :], in_=psb(bs1)[0:nkeys, :], func=AF.Exp,
                                                       scale=0.125), reads=[Bbank[bs1], BE], writes=[BE])
                    Ev = E[:].rearrange("p g (j q) -> p (g j) q", q=128)
                    if mask == "cur":
                        S.op("pool", lambda e: e.affine_select(
                            out=Ev, in_=Ev, pattern=[[0, 8], [1, 128]], compare_op=ALU.is_ge, fill=0.0, base=0,
                            channel_multiplier=-1), reads=[BE], writes=[BE])
                    elif mask == "prev":
                        S.op("pool", lambda e: e.affine_select(
                            out=Ev, in_=Ev, pattern=[[0, 8], [-1, 128]], compare_op=ALU.is_ge, fill=0.0, base=-1,
                            channel_multiplier=1), reads=[BE], writes=[BE])

                scores(kTb[i3], 128, Ecur, BEcur, BkT[i3], "cur")
                has_prev = t >= 2
                if has_prev:
                    scores(kTb[p3], 128, Eprev, BEprev, BkT[p3], "prev")
                scores(kTm, 16, Emeta, BEmeta, BkTm, None)
                bo = [bankB(), bankB()]

                def pv_fn(e):
                    inst = None
                    for h8 in range(8):
                        g, j = h8 // 4, h8 % 4
                        o_ap = psb(bo[g])[:, j * 65:(j + 1) * 65]
                        inst = e.matmul(o_ap, Ecur[:, g, j * 128:(j + 1) * 128], vaug[i3][:, g, :], start=True,
                                        stop=False)
                        if has_prev:
                            inst = e.matmul(o_ap, Eprev[:, g, j * 128:(j + 1) * 128], vaug[p3][:, g, :],
                                            start=False, stop=False)
                        inst = e.matmul(o_ap, Emeta[0:16, g, j * 128:(j + 1) * 128], vmeta[0:16, g, :], start=False,
                                        stop=True)
                    return inst
                rds = [BEcur, BEmeta, Bvaug[i3], Bvmeta] + ([BEprev, Bvaug[p3]] if has_prev else [])
                S.op("pe", pv_fn, reads=rds, writes=[Bbank[bo[0]], Bbank[bo[1]]])
                for g in range(2):
                    ov = psb(bo[g])[:, 0:260].rearrange("p (j d) -> p j d", d=65)
                    S.op("dve", (lambda e, ov=ov, g=g: e.tensor_tensor(
                        out=den[:, 4 * g:4 * g + 4].unsqueeze(2), in0=ov[:, :, 64:65],
                        in1=esink[:, 4 * g:4 * g + 4].unsqueeze(2), op=ALU.add)),
                         reads=[Bbank[bo[g]], B_const, Bden], writes=[Bden])
                S.op("dve", lambda e: e.reciprocal(out=rden[:], in_=den[:]), reads=[Bden], writes=[Brden])
                for g in range(2):
                    ov = psb(bo[g])[:, 0:260].rearrange("p (j d) -> p j d", d=65)
                    S.op("dve", (lambda e, ov=ov, g=g: e.tensor_tensor(
                        out=ao[:, 256 * g:256 * g + 256].rearrange("p (j d) -> p j d", d=64), in0=ov[:, :, 0:64],
                        in1=rden[:, 4 * g:4 * g + 4].unsqueeze(2).to_broadcast([128, 4, 64]), op=ALU.mult)),
                         reads=[Bbank[bo[g]], Brden, Bao], writes=[Bao])

            def do_B2(t):
                i2, i3 = t % 2, t % 3
                if t == 0:
                    return
                bankB = bankB2
                hTt, BhTt = hT[i3], BhT[i3]
                og, Bog, ao, Bao = ogs[i2], Bogs[i2], aos[i2], Baos[i2]
                for gi in range(4):
                    bgt = proj(hTt, BhTt, 2816 + gi * 512, 512, bankB())
                    S.op("act", (lambda e, b=bgt, gi=gi: e.activation(out=thG[:, gi * 512:(gi + 1) * 512],
                                                                      in_=psb(b), func=AF.Tanh, scale=0.5)),
                         reads=[Bbank[bgt]], writes=[BthG])

                def branch(src, Bsrc, w_sb, Bw):
                    transposes(lambda i: src[:, i * 128:(i + 1) * 128], 4, flat3(oT), BoT, Bsrc, "act", bankB())
                    pp = pair()

                    def f(e):
                        inst = None
                        for n in range(2):
                            for c in range(4):
                                inst = e.matmul(psb(pp + n), oT[:, c, :], w_sb[:, c, n * 512:(n + 1) * 512],
                                                start=(c == 0), stop=(c == 3))
                        return inst
                    S.op("pe", f, reads=[BoT, Bw], writes=[Bbank[pp], Bbank[pp + 1]])
                    return pp

                pp1 = branch(og, Bog, wbh_sb, B_wbh)
                for n in range(2):
                    S.op("dve", (lambda e, n=n: e.scalar_tensor_tensor(
                        out=m1[:, n * 512:(n + 1) * 512], in0=thG[:, n * 512:(n + 1) * 512], scalar=1.0,
                        in1=psb(pp1 + n), op0=ALU.add, op1=ALU.mult)),
                         reads=[BthG, Bbank[pp1 + n], Bm1], writes=[Bm1])
                pp2 = branch(ao, Bao, wba_sb, B_wba)
                for n in range(2):
                    S.op("dve", (lambda e, n=n: e.scalar_tensor_tensor(
                        out=r32[:, n * 512:(n + 1) * 512], in0=thG[:, 1024 + n * 512:1024 + (n + 1) * 512],
                        scalar=1.0, in1=psb(pp2 + n), op0=ALU.add, op1=ALU.mult)),
                         reads=[BthG, Bbank[pp2 + n], Br32], writes=[Br32])
                S.op("dve", lambda e: e.tensor_tensor(out=mm[:], in0=m1[:], in1=r32[:], op=ALU.add),
                     reads=[Bm1, Br32], writes=[Bmm])
                transposes(lambda i: mm[:, i * 128:(i + 1) * 128], 8, flat3(mT), BmT, Bmm, "act", bankB())
                pp3 = pair()

                def wo_fn(e):
                    inst = None
                    for n in range(2):
                        for c in range(8):
                            inst = e.matmul(psb(pp3 + n), mT[:, c, :], wo_sb[:, c, n * 512:(n + 1) * 512],
                                            start=(c == 0), stop=(c == 7))
                    return inst
                S.op("pe", wo_fn, reads=[BmT, B_wo], writes=[Bbank[pp3], Bbank[pp3 + 1]])
                for n in range(2):
                    S.op("dve", (lambda e, n=n: e.scalar_tensor_tensor(
                        out=r32[:, n * 512:(n + 1) * 512], in0=psb(pp3 + n), scalar=0.5 / ALPHA,
                        in1=hA[i3][:, n * 512:(n + 1) * 512], op0=ALU.mult, op1=ALU.add)),
                         reads=[Bbank[pp3 + n], BhA[i3], Br32], writes=[Br32])
                ln_stats(stat[1], r32, Br32, r32, Br32, G1, B1, EPS / (ALPHA * ALPHA))
                S.op("act", lambda e: e.copy(out=mm[:], in_=r32[:]), reads=[Br32], writes=[Bmm])
                S.op("sp", lambda e: e.dma_start(out=h1_d[t * 128:(t + 1) * 128, :], in_=r32[:]),
                     reads=[Br32], dma="h1o")
                transposes(lambda i: mm[:, i * 128:(i + 1) * 128], 8, flat3(h1T), Bh1T, Bmm, "act", bankB())
                S.op("sp", lambda e: e.dma_start(out=h1T_d[t], in_=flat3(h1T)), reads=[Bh1T], dma="h1To")

            def merge(la, lb):
                out_, ia, ib = [], 0, 0
                na, nb = len(la), len(lb)
                while ia < na or ib < nb:
                    if ib >= nb or (ia < na and ia * nb <= ib * na):
                        out_.append(la[ia]); ia += 1
                    else:
                        out_.append(lb[ib]); ib += 1
                return out_

            def merge3(ls):
                ls = [l for l in ls if l]
                if not ls:
                    return []
                out_ = ls[0]
                tot = len(ls[0])
                for l in ls[1:]:
                    out_ = merge(out_, l)
                return out_

            order = list(setup_ops)
            S.cur = lst = []
            load_tile(0)
            do_A(0)
            order += lst
            for t in range(NT + 1):
                S.cur = lB2 = []
                if t >= 1:
                    do_B2(t - 1)
                S.cur = lB1 = []
                if t < NT:
                    do_B1(t)
                S.cur = lA = []
                if t + 1 < NT:
                    do_A(t + 1)
                extra = []
                if t >= min(6, NT):
                    take = len(cv_ops) if t == NT else 3
                    extra, cv_ops[:] = cv_ops[:take], cv_ops[take:]
                order += merge3([lB2, lB1, lA, extra]) if PIPELINE else (lB2 + lB1 + lA + extra)
            S.ops = order

            if limit1 is not None:
                S.ops = S.ops[:limit1]
            S.finalize()
            S.emit(nc, block, sems, {k: dsem(k) for k in S.dma_counts})
            final = {e: 0 for e in Sched.ENGS}
            for o in S.ops:
                if o.dma is None and o.milestone:
                    final[o.eng] = max(final[o.eng], o.semval)
            dma_final = {k: 16 * v for k, v in S.dma_counts.items()}

            def sp_fence(e):
                for k, v in final.items():
                    if v > 0:
                        e.wait_ge(sems[k], v)
                for k, v in dma_final.items():
                    e.wait_ge(dsem(k), v)
                e.sem_inc(fence_sem, 1)
            block.sync(sp_fence)
            for en in ("tensor", "scalar", "vector", "gpsimd"):
                getattr(block, en)(lambda e: e.wait_ge(fence_sem, 1))

        S2 = Sched()
        p2 = ExitStack()
        with p2:
            def sb2(name, shape, dt):
                return p2.enter_context(nc.sbuf_tensor(name, list(shape), dt))

            NJ = D_FF // 128
            wfi_sb = sb2("wfi", [128, 8, 2 * D_FF], BF16)
            wfo_sb = sb2("wfo", [128, NJ, D], BF16)
            B_wfi = [[Buf("wfi%d_%d" % (b_, c)) for c in range(8)] for b_ in range(6)]
            B_wfo = [Buf("wfo%d" % j) for j in range(D_FF // 128)]
            G2 = sb2("G2", [128, D], F32); B2 = sb2("B2", [128, D], F32)
            cneg2 = sb2("cneg2", [128, 1], F32)
            Bc2 = Buf("c2")
            hTg = [sb2("hTg%d" % i, [128, 4, 8, 128], BF16) for i in range(2)]
            BhTg = [Buf("hTg%d" % i) for i in range(2)]
            h1in = [sb2("h1in%d" % i, [128, D], F32) for i in range(2)]
            Bh1in = [Buf("h1in%d" % i) for i in range(2)]
            gT = sb2("gT", [128, NJ, 512], BF16)
            BgT = [Buf("gT%d" % j) for j in range(NJ)]
            tha = [sb2("tha%d" % i, [128, 512], BF16) for i in range(2)]
            Btha = [Buf("tha%d" % i) for i in range(2)]
            s2b = [sb2("s2b%d" % i, [128, 512], F32) for i in range(2)]
            Bs2b = [Buf("s2b%d" % i) for i in range(2)]
            rr2 = [sb2("rr2_%d" % i, [128, D], F32) for i in range(2)]
            Brr2 = [Buf("rr2_%d" % i) for i in range(2)]
            st6b = sb2("st6b", [128, 2, 6], F32); Bst6b = Buf("st6b")
            mvb = sb2("mvb", [128, 2], F32); Bmvb = Buf("mvb")
            veb = sb2("veb", [128, 1], F32); Bveb = Buf("veb")
            rsb = sb2("rsb", [128, 1], F32); Brsb = Buf("rsb")
            Bbank2 = [Buf("bank2_%d" % i) for i in range(8)]

            def psb(i):
                return ps[:, i, :]

            wfi_v = wfi_bf.rearrange("(c p) n -> p c n", p=128)
            wfo_v = wfo_bf.rearrange("(j p) n -> p j n", p=128)
            slotB = [Buf("wslot%d" % i) for i in range(8)]
            kk = 0
            for blk in (0, 2, 3, 1, 4, 5):
                col0 = blk * 1024
                n = min(1024, 2 * D_FF - col0)
                for c in range(8):
                    S2.op("sp", (lambda e, c=c, col0=col0, n=n: e.dma_start(out=wfi_sb[:, c, col0:col0 + n],
                                                                            in_=wfi_v[:, c, col0:col0 + n])),
                          writes=[B_wfi[blk][c], slotB[kk % 8]], dma="wt%d" % (kk % 8))
                    kk += 1
            for j in range(NJ):
                S2.op("sp", (lambda e, j=j: e.dma_start(out=wfo_sb[:, j, :], in_=wfo_v[:, j, :])),
                      writes=[B_wfo[j], slotB[kk % 8]], dma="wt%d" % (kk % 8))
                kk += 1
            S2.op("sp", lambda e: e.dma_start(out=G2[:], in_=ln2_g.partition_broadcast(128)), writes=[Bc2], dma="g2")
            S2.op("sp", lambda e: e.dma_start(out=B2[:], in_=ln2_b.partition_broadcast(128)), writes=[Bc2], dma="b2")
            S2.op("dve", lambda e: e.memset(cneg2[:], -0.5), writes=[Bc2])

            rr2c = {"ab": 0}
            tiles2 = list(range(1, NT))
            groups = [tiles2[i:i + 4] for i in range(0, len(tiles2), 4)]

            def load_group(g):
                gi = g % 2
                for i, t in enumerate(groups[g]):
                    S2.op("sp", (lambda e, gi=gi, i=i, t=t: e.dma_start(
                        out=hTg[gi][:, i, :, :].rearrange("p c q -> p (c q)"), in_=h1T_d[t])),
                          writes=[BhTg[gi]], dma="hTg%d_%d" % (gi, i))

            def load_h1(n):
                fi = n % 2
                t = tiles2[n]
                S2.op("sp", lambda e: e.dma_start(out=h1in[fi][:], in_=h1_d[t * 128:(t + 1) * 128, :]),
                      writes=[Bh1in[fi]], dma="h1in%d" % fi)

            def ffn_in(g, j):
                gi = g % 2
                ntl = len(groups[g])
                ntok = ntl * 128
                ba = 2 * (rr2c["ab"] % 2)
                rr2c["ab"] += 1
                bu = ba + 1
                ai = j % 2

                def ff_in(e):
                    inst = None
                    for c in range(8):
                        inst = e.matmul(psb(ba)[:, 0:ntok], wfi_sb[:, c, j * 128:(j + 1) * 128],
                                        hTg[gi][:, 0:ntl, c, :], start=(c == 0), stop=(c == 7))
                    for c in range(8):
                        inst = e.matmul(psb(bu)[:, 0:ntok], wfi_sb[:, c, D_FF + j * 128:D_FF + (j + 1) * 128],
                                        hTg[gi][:, 0:ntl, c, :], start=(c == 0), stop=(c == 7))
                    return inst
                S2.op("pe", ff_in, reads=[BhTg[gi]] + B_wfi[j // 8] + B_wfi[(D_FF + j * 128) // 1024],
                      writes=[Bbank2[ba], Bbank2[bu]])
                S2.op("act", lambda e: e.activation(out=tha[ai][:, 0:ntok], in_=psb(ba)[:, 0:ntok], func=AF.Tanh,
                                                    scale=0.5),
                      reads=[Bbank2[ba]], writes=[Btha[ai]])
                S2.op("dve", lambda e: e.scalar_tensor_tensor(
                    out=s2b[ai][:, 0:ntok], in0=tha[ai][:, 0:ntok], scalar=1.0, in1=psb(ba)[:, 0:ntok],
                    op0=ALU.add, op1=ALU.mult),
                      reads=[Btha[ai], Bbank2[ba]], writes=[Bs2b[ai]])
                S2.op("dve", lambda e: e.tensor_tensor(
                    out=gT[:, j, 0:ntok], in0=s2b[ai][:, 0:ntok], in1=psb(bu)[:, 0:ntok], op=ALU.mult),
                      reads=[Bs2b[ai], Bbank2[bu]], writes=[BgT[j]])

            def ffn_out(n, i):
                t = tiles2[n]
                fi = n % 2
                pp = 4 + 2 * fi
                rb, Brb = rr2[fi], Brr2[fi]

                def ff_out(e):
                    inst = None
                    for nn in range(2):
                        for j in range(NJ):
                            inst = e.matmul(psb(pp + nn), gT[:, j, i * 128:(i + 1) * 128],
                                            wfo_sb[:, j, nn * 512:(nn + 1) * 512], start=(j == 0),
                                            stop=(j == NJ - 1))
                    return inst
                S2.op("pe", ff_out, reads=BgT + B_wfo, writes=[Bbank2[pp], Bbank2[pp + 1]])
                for nn in range(2):
                    S2.op("dve", (lambda e, nn=nn: e.scalar_tensor_tensor(
                        out=rb[:, nn * 512:(nn + 1) * 512], in0=psb(pp + nn), scalar=0.5 / ALPHA,
                        in1=h1in[fi][:, nn * 512:(nn + 1) * 512], op0=ALU.mult, op1=ALU.add)),
                          reads=[Bbank2[pp + nn], Bh1in[fi], Brb], writes=[Brb])

                def st_f(e):
                    e.bn_stats(out=st6b[:, 0, :], in_=rb[:, 0:512])
                    return e.bn_stats(out=st6b[:, 1, :], in_=rb[:, 512:1024])
                S2.op("dve", st_f, reads=[Brb], writes=[Bst6b])
                S2.op("dve", lambda e: e.bn_aggr(out=mvb[:], in_=st6b[:].rearrange("p a b -> p (a b)")),
                      reads=[Bst6b], writes=[Bmvb])
                S2.op("pool", lambda e: e.tensor_scalar(out=veb[:], in0=mvb[:, 1:2], scalar1=EPS / (ALPHA * ALPHA),
                                                        scalar2=None, op0=ALU.add), reads=[Bmvb], writes=[Bveb])
                S2.op("pool", lambda e: e.tensor_tensor(out=rsb[:], in0=veb[:], in1=cneg2[:, 0:1], op=ALU.pow),
                      reads=[Bveb, Bc2], writes=[Brsb])
                S2.op("dve", lambda e: e.scalar_tensor_tensor(out=rb[:], in0=rb[:], scalar=mvb[:, 0:1], in1=G2[:],
                                                              op0=ALU.subtract, op1=ALU.mult),
                      reads=[Brb, Bmvb, Bc2], writes=[Brb])
                S2.op("dve", lambda e: e.scalar_tensor_tensor(out=rb[:], in0=rb[:], scalar=rsb[:, 0:1], in1=B2[:],
                                                              op0=ALU.mult, op1=ALU.add),
                      reads=[Brb, Brsb, Bc2], writes=[Brb])
                S2.op("sp", lambda e: e.dma_start(out=out[(t - 1) * 128:t * 128, :], in_=rb[:]),
                      reads=[Brb], dma="out%d" % fi)

            if groups:
                load_group(0)
                load_h1(0)
            nflat = 0
            for g in range(len(groups)):
                if g + 1 < len(groups):
                    load_group(g + 1)
                for j in range(NJ):
                    ffn_in(g, j)
                for i in range(len(groups[g])):
                    if nflat + 1 < len(tiles2):
                        load_h1(nflat + 1)
                    ffn_out(nflat, i)
                    nflat += 1

            if skip2:
                S2.ops = []
            if limit2 is not None:
                S2.ops = S2.ops[:limit2]
            S2.finalize()
            S2.emit(nc, block, sems2, {k: dsem("p2_" + k) for k in S2.dma_counts})

            def sp_end(e):
                for k, v in S2.dma_counts.items():
                    e.wait_ge(dsem("p2_" + k), 16 * v)
            block.sync(sp_end)
    return nc


def _const_tables():
    half = 32
    inv = 10000.0 ** (-np.arange(half, dtype=np.float32) / half)
    pos = (np.arange(NTILES * 128, dtype=np.int32) - PAD).astype(np.float32)
    ang = pos[:, None] * inv[None, :]
    cos = np.cos(ang).astype(np.float32)
    sin = np.sin(ang).astype(np.float32)
    rope_t = np.concatenate([cos, cos, -sin, sin], axis=1).astype(np.float32)
    s = np.arange(128)
    same = (s[:, None] // 64) == (s[None, :] // 64)
    T64 = (same & (s[:, None] <= s[None, :])).astype(np.float32)
    U64 = (same & (s[:, None] > s[None, :])).astype(np.float32)
    ident = np.eye(128, dtype=np.float32)
    cmat = np.concatenate([T64, U64, ident], axis=1).astype(np.float32)
    return rope_t, cmat


def _in_maps(inputs, cores):
    rope_t, cmat = _const_tables()
    f = lambda a: np.ascontiguousarray(np.asarray(a, dtype=np.float32))
    shared = {
        "meta": f(inputs["meta_tokens"]),
        "ln_emb_g": f(inputs["ln_emb_g"]).reshape(1, D),
        "ln_emb_b": f(inputs["ln_emb_b"]).reshape(1, D),
        "w_in": f(inputs["w_in"])[0],
        "hg_lb": f(inputs["hg_lower_bounds"]).reshape(1, 1024),
        "hg_ng": f(inputs["hg_norm_g"]).reshape(1, 128),
        "sinks": f(inputs["attn_sinks"]).reshape(1, 8),
        "w_bhg": f(inputs["w_branch_hg"])[0],
        "w_batt": f(inputs["w_branch_attn"])[0],
        "w_out": f(inputs["w_out"])[0],
        "ln1_g": f(inputs["ln1_g"]).reshape(1, D),
        "ln1_b": f(inputs["ln1_b"]).reshape(1, D),
        "w_fi": f(inputs["w_ffn_in"])[0],
        "w_fo": f(inputs["w_ffn_out"])[0],
        "ln2_g": f(inputs["ln2_g"]).reshape(1, D),
        "ln2_b": f(inputs["ln2_b"]).reshape(1, D),
        "rope_t": rope_t,
        "cmat": cmat,
    }
    xs = np.asarray(inputs["x"], dtype=np.float32)
    return [dict(shared, x=np.ascontiguousarray(xs[b])) for b in cores]


def kernel(**inputs):
    nc = build(NTILES)
    in_maps = _in_maps(inputs, list(range(NCORES)))
    res = run_bass_kernel_spmd(nc, in_maps, core_ids=list(range(NCORES)))
    return np.stack([np.asarray(r["y_out"], dtype=np.float32) for r in res.results], axis=0)
```

```python
import math
from contextlib import ExitStack

import numpy as np
import concourse.bass as bass
import concourse.mybir as mybir
from concourse.bass_utils import run_bass_kernel_spmd

F32 = mybir.dt.float32
BF16 = mybir.dt.bfloat16
AF = mybir.ActivationFunctionType
ALU = mybir.AluOpType

D = 1024
SEQ = 8192
NTILES = SEQ // 128 + 1
N_META = 16
PAD = 112
IN_W = 4864
D_FF = 2816
EPS = 1e-5
ALPHA = 2.0 ** 0.25
NCORES = 8
DMA_SCRATCH = 4096
PREFETCH = True
PIPELINE = True
VAR = set()


class Buf:
    __slots__ = ("name", "w", "r")

    def __init__(self, name):
        self.name = name
        self.w = None
        self.r = []


class Op:
    __slots__ = ("eng", "fn", "deps", "idx", "milestone", "semval", "dma", "dmacount", "reads", "writes")

    def __init__(self, eng, fn, dma, reads, writes):
        self.eng = eng
        self.fn = fn
        self.deps = {}
        self.milestone = False
        self.semval = None
        self.dma = dma
        self.dmacount = None
        self.reads = list(reads)
        self.writes = list(writes)


class Sched:
    ENGS = ("pe", "act", "dve", "pool", "sp")

    def __init__(self):
        self.ops = []
        self.cur = None
        self.dma_counts = {}

    def op(self, eng, fn, reads=(), writes=(), dma=None):
        o = Op(eng, fn, dma, reads, writes)
        (self.cur if self.cur is not None else self.ops).append(o)
        return o

    def resolve(self):
        self.dma_counts = {}
        for o in self.ops:
            reads, writes = o.reads, o.writes
            wset = set(id(b) for b in writes)
            for b in reads:
                if b.w is not None:
                    o.deps[id(b.w)] = (b.w, "RAW")
            for b in writes:
                if b.w is not None and id(b.w) not in o.deps:
                    o.deps[id(b.w)] = (b.w, "WAW")
                for r in b.r:
                    if id(r) not in o.deps:
                        o.deps[id(r)] = (r, "WAR")
            for b in writes:
                b.w = o
                b.r = []
            for b in reads:
                if id(b) not in wset:
                    b.r.append(o)
            if o.dma is not None:
                self.dma_counts[o.dma] = self.dma_counts.get(o.dma, 0) + 1
                o.dmacount = self.dma_counts[o.dma]

    @staticmethod
    def _needed(o, d, kind):
        if d.dma is not None:
            return True
        if d.eng == o.eng and kind != "RAW":
            return False
        return True

    def finalize(self):
        self.resolve()
        for o in self.ops:
            for d, kind in o.deps.values():
                if d.dma is None and self._needed(o, d, kind):
                    d.milestone = True
        last = {}
        for o in self.ops:
            if o.dma is None:
                last[o.eng] = o
        for o in last.values():
            o.milestone = True
        cnt = {e: 0 for e in self.ENGS}
        for o in self.ops:
            if o.dma is None and o.milestone:
                cnt[o.eng] += 1
                o.semval = cnt[o.eng]

    def emit(self, nc, block, sems, dma_sems):
        engobj = {"pe": "tensor", "act": "scalar", "dve": "vector", "pool": "gpsimd", "sp": "sync"}
        for eng in self.ENGS:
            myops = [o for o in self.ops if o.eng == eng]
            if not myops:
                continue

            def body(e, myops=myops, eng=eng):
                waited = {}
                for o in myops:
                    need = {}
                    for d, kind in o.deps.values():
                        if not self._needed(o, d, kind):
                            continue
                        if d.dma is not None:
                            key = ("d", d.dma)
                            val = 16 * d.dmacount
                        else:
                            key = ("e", d.eng)
                            val = d.semval
                        if val > need.get(key, 0):
                            need[key] = val
                    for key, val in need.items():
                        if waited.get(key, 0) >= val:
                            continue
                        waited[key] = val
                        sem = dma_sems[key[1]] if key[0] == "d" else sems[key[1]]
                        e.wait_ge(sem, val)
                    inst = o.fn(e)
                    if o.dma is not None:
                        inst.then_inc(dma_sems[o.dma], 16)
                    elif o.milestone:
                        inst.then_inc(sems[eng], 1)

            getattr(block, engobj[eng])(body)


def build(NT=NTILES, dbg=False, limit1=None, skip2=False, limit2=None):
    nc = bass.Bass("TRN2", target_bir_lowering=False, dynamic_dma_scratch_size=DMA_SCRATCH)
    NG2 = (NT - 1 + 3) // 4
    NOUT = (NT - 1) * 128

    def din(name, shape, dt=F32):
        return nc.dram_tensor(name, list(shape), dt, kind="ExternalInput").ap()

    x = din("x", [SEQ, D])
    meta = din("meta", [N_META, D])
    ln_emb_g = din("ln_emb_g", [1, D])
    ln_emb_b = din("ln_emb_b", [1, D])
    w_in = din("w_in", [D, IN_W])
    hg_lb = din("hg_lb", [1, 1024])
    hg_ng = din("hg_ng", [1, 128])
    sinks = din("sinks", [1, 8])
    w_bhg = din("w_bhg", [512, D])
    w_batt = din("w_batt", [512, D])
    w_out = din("w_out", [D, D])
    ln1_g = din("ln1_g", [1, D])
    ln1_b = din("ln1_b", [1, D])
    w_fi = din("w_fi", [D, 2 * D_FF])
    w_fo = din("w_fo", [D_FF, D])
    ln2_g = din("ln2_g", [1, D])
    ln2_b = din("ln2_b", [1, D])
    rope_t = din("rope_t", [NTILES * 128, 128])
    cmat = din("cmat", [128, 3 * 128])
    out = nc.dram_tensor("y_out", [NOUT, D], F32, kind="ExternalOutput").ap()
    h1_d = nc.dram_tensor("h1_scr", [NTILES * 128, D], F32, kind="Internal").ap()
    h1T_d = nc.dram_tensor("h1T_scr", [NTILES, 128, 1024], BF16, kind="Internal").ap()
    wfi_bf = nc.dram_tensor("wfi_bf", [D, 2 * D_FF], BF16, kind="Internal").ap()
    wfo_bf = nc.dram_tensor("wfo_bf", [D_FF, D], BF16, kind="Internal").ap()

    es = ExitStack()
    with es:
        def sb(name, shape, dt):
            return es.enter_context(nc.sbuf_tensor(name, list(shape), dt))

        sem_names = list(Sched.ENGS)
        sems = {e: es.enter_context(nc.semaphore("s_" + e)) for e in sem_names}
        sems2 = {e: es.enter_context(nc.semaphore("t_" + e)) for e in sem_names}
        fence_sem = es.enter_context(nc.semaphore("fence"))
        dma_sem_store = {}

        def dsem(name):
            if name not in dma_sem_store:
                dma_sem_store[name] = es.enter_context(nc.semaphore("d_" + name))
            return dma_sem_store[name]

        ps = es.enter_context(nc.psum_tensor("ps", [128, 8, 512], F32))
        block = es.enter_context(nc.Block())

        S = Sched()
        p1 = ExitStack()
        with p1:
            def sb1(name, shape, dt):
                return p1.enter_context(nc.sbuf_tensor(name, list(shape), dt))

            win_sb = sb1("win", [128, 8, IN_W], BF16)
            wbh_sb = sb1("wbh", [128, 4, D], BF16)
            wba_sb = sb1("wba", [128, 4, D], BF16)
            wo_sb = sb1("wo", [128, 8, D], BF16)
            B_win = [Buf("win%d" % c) for c in range(5)]
            B_wbh, B_wba, B_wo = Buf("wbh"), Buf("wba"), Buf("wo")
            Gemb = sb1("Gemb", [128, D], F32); Bemb = sb1("Bemb", [128, D], F32)
            G1 = sb1("G1", [128, D], F32); B1 = sb1("B1", [128, D], F32)
            C2 = sb1("C2", [128, 512], F32)
            ngcol = sb1("ngcol", [128, 1], F32)
            cm = sb1("cm", [128, 3, 128], F32)
            mask64 = sb1("mask64", [128, 128], BF16)
            ident = sb1("ident", [128, 128], BF16)
            esink = sb1("esink", [128, 8], F32)
            cneg = sb1("cneg", [128, 8], F32)
            vmask = sb1("vmask", [128, 1], F32)
            B_const = Buf("const")
            def dbl(name, shape, dt, n=2):
                return ([sb1("%s_%d" % (name, i), shape, dt) for i in range(n)],
                        [Buf("%s_%d" % (name, i)) for i in range(n)])

            x32, Bx32 = dbl("x32", [128, D], F32)
            hA, BhA = dbl("hA", [128, D], F32, 3)
            rt, Brt = dbl("rt", [128, 128], F32)
            stat = []
            for i in range(2):
                stat.append(dict(st6=sb1("st6_%d" % i, [128, 2, 6], F32), Bst6=Buf("st6"),
                                 mv=sb1("mv_%d" % i, [128, 2], F32), Bmv=Buf("mv"),
                                 ve=sb1("ve_%d" % i, [128, 1], F32), Bve=Buf("ve"),
                                 rs=sb1("rs_%d" % i, [128, 1], F32), Brs=Buf("rs")))
            hbf = sb1("hbf", [128, D], BF16); Bhbf = Buf("hbf")
            hT, BhT = dbl("hT", [128, 8, 128], BF16, 3)
            q2 = sb1("q2", [128, 512], F32); Bq2 = Buf("q2")
            k32 = sb1("k32", [128, 512], F32); Bk32 = Buf("k32")
            lnf = sb1("lnf", [128, 512], F32); Blnf = Buf("lnf")
            eb = [sb1("eb%d" % i, [128, 512], F32) for i in range(3)]
            Beb = [Buf("eb%d" % i) for i in range(3)]
            qt = sb1("qt", [128, 512], BF16); Bqt = Buf("qt")
            kt = sb1("kt", [128, 512], BF16); Bkt = Buf("kt")
            thq, Bthq, thf, Bthf = qt, Bqt, kt, Bkt
            kb, Bkb = dbl("kb", [128, 512], BF16)
            qkT, BqkT = dbl("qkT", [128, 8, 128], BF16)
            A_sb = sb1("A_sb", [128, 4, 128], BF16); BA_sb = Buf("A_sb")
            v_bf, Bv_bf = dbl("v_bf", [128, 512], BF16)
            gn2, Bgn2 = dbl("gn2", [128, 512], F32)
            ogs, Bogs = dbl("og", [128, 512], BF16)
            oT = sb1("oT", [128, 4, 128], BF16); BoT = Buf("oT")
            S32 = sb1("S32", [128, 4, 128], F32); BS32 = Buf("S32")
            Sbf = sb1("Sbf", [128, 4, 128], BF16); BSbf = Buf("Sbf")
            dec, Bdec = dbl("dec", [128, 8], F32)
            ss = sb1("ss", [128, 4], F32); Bss = Buf("ss")
            rsn = sb1("rsn", [128, 4], F32); Brsn = Buf("rsn")
            qr = sb1("qr", [128, 512], BF16); Bqr = Buf("qr")
            thg, Bthg = qr, Bqr
            kr = sb1("kr", [128, 128], BF16); Bkr = Buf("kr")
            qT, BqT = dbl("qT", [128, 4, 128], BF16)
            kTb, BkT = dbl("kT", [128, 128], BF16, 3)
            kTm = sb1("kTm", [128, 16], BF16); BkTm = Buf("kTm")
            vaug, Bvaug = dbl("vaug", [128, 2, 65], BF16, 3)
            vmeta = sb1("vmeta", [16, 2, 65], BF16); Bvmeta = Buf("vmeta")
            Ecur = sb1("Ecur", [128, 2, 512], BF16); BEcur = Buf("Ecur")
            Eprev = sb1("Eprev", [128, 2, 512], BF16); BEprev = Buf("Eprev")
            Emeta = sb1("Emeta", [16, 2, 512], BF16); BEmeta = Buf("Emeta")
            den = sb1("den", [128, 8], F32); Bden = Buf("den")
            rden = sb1("rden", [128, 8], F32); Brden = Buf("rden")
            aos, Baos = dbl("ao", [128, 512], BF16)
            thG = sb1("thG", [128, 2048], BF16); BthG = Buf("thG")
            m1 = sb1("m1", [128, D], BF16); Bm1 = Buf("m1")
            mm = sb1("mm", [128, D], BF16); Bmm = Buf("mm")
            mT = sb1("mT", [128, 8, 128], BF16); BmT = Buf("mT")
            r32 = sb1("r32", [128, D], F32); Br32 = Buf("r32")
            h1T = sb1("h1T", [128, 8, 128], BF16); Bh1T = Buf("h1T")

            Bbank = [Buf("bank%d" % i) for i in range(8)]
            rr = {"a": 0, "b": 0, "p": 0}

            def bankA():
                i = rr["a"]
                rr["a"] = (i + 1) % 3
                return i

            B1SEQ = [3, 4, 5, 4]
            B1CYC = [3, 5, 4]

            def bankB1():
                i = rr["b"]
                rr["b"] = i + 1
                return B1SEQ[i] if i < 4 else B1CYC[(i - 4) % 3]

            def bankB2():
                i = rr["p"]
                rr["p"] = (i + 1) % 2
                return 6 + i

            def pair():
                return 6

            def psb(i):
                return ps[:, i, :]

            def psb16(i):
                return ps[:, i, :].bitcast(BF16)

            setup_ops = []
            S.cur = setup_ops

            def bc_load(dst, src_row, n):
                return lambda e: e.dma_start(out=dst, in_=src_row.partition_broadcast(128))

            S.op("sp", lambda e: e.dma_start(out=cm[:].rearrange("p a b -> p (a b)"), in_=cmat),
                 writes=[B_const], dma="c0")
            S.op("sp", bc_load(Gemb[:], ln_emb_g, D), writes=[B_const], dma="c1")
            S.op("sp", bc_load(Bemb[:], ln_emb_b, D), writes=[B_const], dma="c2")
            S.op("sp", bc_load(G1[:], ln1_g, D), writes=[B_const], dma="c3")
            S.op("sp", bc_load(B1[:], ln1_b, D), writes=[B_const], dma="c4")
            S.op("sp", bc_load(eb[0][:], hg_lb[:, 0:512], 512), writes=[Beb[0]], dma="c5a")
            S.op("sp", bc_load(eb[1][:], hg_lb[:, 512:1024], 512), writes=[Beb[1]], dma="c5b")
            S.op("sp", bc_load(esink[:], sinks, 8), writes=[B_const], dma="c6")
            S.op("sp", lambda e: e.dma_start(out=ngcol[:], in_=hg_ng.rearrange("a p -> p a")), writes=[B_const],
                 dma="c7")
            S.op("dve", lambda e: e.tensor_copy(out=mask64[:], in_=cm[:, 0, :]), reads=[B_const], writes=[B_const])
            S.op("dve", lambda e: e.tensor_copy(out=ident[:], in_=cm[:, 2, :]), reads=[B_const], writes=[B_const])
            S.op("dve", lambda e: e.memset(cneg[:], -0.5), writes=[B_const])
            S.op("dve", lambda e: e.memset(vmask[:], 1.0), writes=[B_const])
            S.op("pool", lambda e: e.affine_select(out=vmask[:], in_=vmask[:], pattern=[[0, 1]],
                                                   compare_op=ALU.is_ge, fill=0.0, base=-PAD,
                                                   channel_multiplier=1),
                 reads=[B_const], writes=[B_const])
            S.op("dve", lambda e: e.tensor_tensor(out=C2[:], in0=eb[0][:], in1=eb[1][:], op=ALU.subtract),
                 reads=[Beb[0], Beb[1]], writes=[B_const])
            S.op("act", lambda e: e.activation(out=C2[:], in_=C2[:], func=AF.Tanh, scale=0.5),
                 reads=[B_const], writes=[B_const])
            S.op("dve", lambda e: e.tensor_scalar(out=C2[:], in0=C2[:], scalar1=-0.25, scalar2=0.25,
                                                  op0=ALU.mult, op1=ALU.add),
                 reads=[B_const], writes=[B_const])
            S.op("act", lambda e: e.activation(out=esink[:], in_=esink[:], func=AF.Exp),
                 reads=[B_const], writes=[B_const])
            S.op("dve", lambda e: e.memset(S32[:], 0.0), writes=[BS32])
            S.op("dve", lambda e: e.memset(Sbf[:], 0.0), writes=[BSbf])
            for i in range(3):
                S.op("dve", (lambda e, i=i: e.memset(vaug[i][:], 1.0)), writes=[Bvaug[i]])

            stage1 = [(x32[0], Bx32[0]), (x32[1], Bx32[1]), (hA[0], BhA[0]), (hA[1], BhA[1]), (r32, Br32),
                      (hA[2], BhA[2])]
            wl = {"i": 0}

            def load_w(Sx, stages, dst_ap, src_ap, n, Bdst, tag, scale_col=None):
                k = wl["i"]
                wl["i"] += 1
                stg, Bstg = stages[k % len(stages)]
                ce = ("act", "dve")[k % 2]
                Sx.op("sp", lambda e: e.dma_start(out=stg[:, 0:n], in_=src_ap), writes=[Bstg],
                      dma="%s%d" % (tag, k % len(stages)))
                if scale_col is not None:
                    Sx.op("dve", lambda e: e.tensor_scalar(out=dst_ap, in0=stg[:, 0:n], scalar1=scale_col[:, 0:1],
                                                           scalar2=0.5, op0=ALU.mult, op1=ALU.mult),
                          reads=[Bstg, B_const], writes=[Bdst])
                elif ce == "act":
                    Sx.op("act", lambda e: e.copy(out=dst_ap, in_=stg[:, 0:n]), reads=[Bstg], writes=[Bdst])
                else:
                    Sx.op(ce, lambda e: e.tensor_copy(out=dst_ap, in_=stg[:, 0:n]), reads=[Bstg], writes=[Bdst])

            w_in_v = w_in.rearrange("(c p) n -> p c n", p=128)
            for col0 in range(0, IN_W, 1024):
                for c in range(8):
                    n = min(1024, IN_W - col0)
                    load_w(S, stage1, win_sb[:, c, col0:col0 + n], w_in_v[:, c, col0:col0 + n], n,
                           B_win[col0 // 1024], "ws")
            wbh_v = w_bhg.rearrange("(c p) n -> p c n", p=128)
            wba_v = w_batt.rearrange("(c p) n -> p c n", p=128)
            wo_v = w_out.rearrange("(c p) n -> p c n", p=128)
            for c in range(4):
                load_w(S, stage1, wbh_sb[:, c, :], wbh_v[:, c, :], 1024, B_wbh, "ws", scale_col=ngcol)
            for c in range(4):
                load_w(S, stage1, wba_sb[:, c, :], wba_v[:, c, :], 1024, B_wba, "ws")
            for c in range(8):
                load_w(S, stage1, wo_sb[:, c, :], wo_v[:, c, :], 1024, B_wo, "ws")

            cvB = [Buf("cv%d" % i) for i in range(2)]
            ncv = 0
            cv_ops = []
            S.cur = cv_ops
            for r0 in range(0, D, 128):
                S.op("pool", (lambda e, r0=r0: e.dma_start(out=wfi_bf[r0:r0 + 128, :], in_=w_fi[r0:r0 + 128, :])),
                     reads=[B_wo], writes=[cvB[ncv % 2]], dma="cv%d" % (ncv % 2))
                ncv += 1
            for r0 in range(0, D_FF, 128):
                S.op("pool", (lambda e, r0=r0: e.dma_start(out=wfo_bf[r0:r0 + 128, :], in_=w_fo[r0:r0 + 128, :])),
                     writes=[cvB[ncv % 2]], dma="cv%d" % (ncv % 2))
                ncv += 1

            S.cur = setup_ops
            T64 = cm[:, 0, :]
            U64 = cm[:, 1, :]

            def transposes(src_fn, n, dst2, BdstT, Bsrc, evac_eng, bk):
                pv = psb16(bk)

                def pe_fn(e):
                    inst = None
                    for i in range(n):
                        inst = e.transpose(pv[:, i * 128:(i + 1) * 128], src_fn(i), ident[:])
                    return inst
                S.op("pe", pe_fn, reads=[Bsrc, B_const], writes=[Bbank[bk]])
                if evac_eng == "act":
                    S.op("act", lambda e: e.copy(out=dst2, in_=pv[:, 0:n * 128]), reads=[Bbank[bk]], writes=[BdstT])
                else:
                    S.op("dve", lambda e: e.tensor_copy(out=dst2, in_=pv[:, 0:n * 128]), reads=[Bbank[bk]],
                         writes=[BdstT])

            def flat3(tl):
                return tl[:].rearrange("p a b -> p (a b)")

            def proj(hTt, BhTt, col0, ncols, bk):
                def pe_fn(e):
                    inst = None
                    for c in range(8):
                        inst = e.matmul(psb(bk)[:, 0:ncols], hTt[:, c, :], win_sb[:, c, col0:col0 + ncols],
                                        start=(c == 0), stop=(c == 7))
                    return inst
                S.op("pe", pe_fn, reads=[BhTt] + B_win[col0 // 1024:(col0 + ncols - 1) // 1024 + 1],
                     writes=[Bbank[bk]])
                return bk

            def ln_stats(sd, src, Bsrc, dst, Bdst, Gt, Bt, eps):
                st6, mv, ve, rs = sd["st6"], sd["mv"], sd["ve"], sd["rs"]

                def f(e):
                    e.bn_stats(out=st6[:, 0, :], in_=src[:, 0:512])
                    return e.bn_stats(out=st6[:, 1, :], in_=src[:, 512:1024])
                S.op("dve", f, reads=[Bsrc], writes=[sd["Bst6"]])
                S.op("dve", lambda e: e.bn_aggr(out=mv[:], in_=st6[:].rearrange("p a b -> p (a b)")),
                     reads=[sd["Bst6"]], writes=[sd["Bmv"]])
                S.op("pool", lambda e: e.tensor_scalar(out=ve[:], in0=mv[:, 1:2], scalar1=eps, scalar2=None,
                                                       op0=ALU.add), reads=[sd["Bmv"]], writes=[sd["Bve"]])
                S.op("pool", lambda e: e.tensor_tensor(out=rs[:], in0=ve[:], in1=cneg[:, 0:1], op=ALU.pow),
                     reads=[sd["Bve"], B_const], writes=[sd["Brs"]])
                S.op("dve", lambda e: e.scalar_tensor_tensor(out=src[:], in0=src[:], scalar=mv[:, 0:1],
                                                             in1=Gt[:], op0=ALU.subtract, op1=ALU.mult),
                     reads=[Bsrc, sd["Bmv"], B_const], writes=[Bsrc])
                S.op("dve", lambda e: e.scalar_tensor_tensor(out=dst[:], in0=src[:], scalar=rs[:, 0:1],
                                                             in1=Bt[:], op0=ALU.mult, op1=ALU.add),
                     reads=[Bsrc, sd["Brs"], B_const] + ([Bdst] if Bdst is not Bsrc else []), writes=[Bdst])

            def load_tile(t):
                i2 = t % 2
                xb, Bxb = x32[i2], Bx32[i2]
                if t == 0:
                    S.op("pool", lambda e: e.memset(xb[:], 0.0), writes=[Bxb])
                    S.op("sp", lambda e: e.dma_start(out=xb[PAD:128, :], in_=meta), reads=[Bxb], writes=[Bxb],
                         dma="x0")
                else:
                    S.op("sp", lambda e: e.dma_start(out=xb[:], in_=x[(t - 1) * 128:t * 128, :]),
                         writes=[Bxb], dma="x%d" % i2)
                S.op("sp", lambda e: e.dma_start(out=rt[i2][:], in_=rope_t[t * 128:(t + 1) * 128, :]),
                     writes=[Brt[i2]], dma="rt%d" % i2)

            def do_A(t):
                i2, i3 = t % 2, t % 3
                xb, Bxb = x32[i2], Bx32[i2]
                hTt, BhTt = hT[i3], BhT[i3]
                if t + 1 < NT:
                    load_tile(t + 1)
                ln_stats(stat[0], xb, Bxb, hA[i3], BhA[i3], Gemb, Bemb, EPS)
                if t == 0:
                    S.op("dve", lambda e: e.tensor_scalar(out=hA[i3][:], in0=hA[i3][:], scalar1=vmask[:, 0:1],
                                                          scalar2=None, op0=ALU.mult),
                         reads=[BhA[i3], B_const], writes=[BhA[i3]])
                S.op("act", lambda e: e.copy(out=hbf[:], in_=hA[i3][:]), reads=[BhA[i3]], writes=[Bhbf])
                transposes(lambda i: hbf[:, i * 128:(i + 1) * 128], 8, flat3(hTt), BhTt, Bhbf, "act", bankA())

                bq = proj(hTt, BhTt, 0, 512, bankA())
                S.op("act", lambda e: e.activation(out=thq[:], in_=psb(bq), func=AF.Tanh, scale=0.5),
                     reads=[Bbank[bq]], writes=[Bthq])
                S.op("dve", lambda e: e.scalar_tensor_tensor(out=q2[:], in0=thq[:], scalar=1.0, in1=psb(bq),
                                                             op0=ALU.add, op1=ALU.mult),
                     reads=[Bthq, Bbank[bq]], writes=[Bq2])
                bf_ = proj(hTt, BhTt, 512, 512, bankA())
                S.op("act", lambda e: e.activation(out=thf[:], in_=psb(bf_), func=AF.Tanh, scale=-0.5),
                     reads=[Bbank[bf_]], writes=[Bthf])
                S.op("dve", lambda e: e.scalar_tensor_tensor(out=k32[:], in0=thf[:], scalar=1.0, in1=C2[:],
                                                             op0=ALU.add, op1=ALU.mult),
                     reads=[Bthf, B_const], writes=[Bk32])
                if t == 0:
                    S.op("dve", lambda e: e.tensor_scalar(out=k32[:], in0=k32[:], scalar1=vmask[:, 0:1], scalar2=None,
                                                          op0=ALU.mult), reads=[Bk32, B_const], writes=[Bk32])
                bv = proj(hTt, BhTt, 1024, 512, bankA())
                S.op("act", lambda e: e.copy(out=v_bf[i2][:], in_=psb(bv)), reads=[Bbank[bv]], writes=[Bv_bf[i2]])
                bg = proj(hTt, BhTt, 1536, 512, bankA())
                S.op("act", lambda e: e.activation(out=thg[:], in_=psb(bg), func=AF.Tanh, scale=0.5),
                     reads=[Bbank[bg]], writes=[Bthg])
                S.op("dve", lambda e: e.scalar_tensor_tensor(out=gn2[i2][:], in0=thg[:], scalar=1.0, in1=psb(bg),
                                                             op0=ALU.add, op1=ALU.mult),
                     reads=[Bthg, Bbank[bg]], writes=[Bgn2[i2]])

                S.op("act", lambda e: e.activation(out=lnf[:], in_=k32[:], func=AF.Ln, scale=-1.0, bias=1.0),
                     reads=[Bk32], writes=[Blnf])
                bP = bankA()
                S.op("pe", lambda e: e.matmul(psb(bP), T64, lnf[:], start=True, stop=True),
                     reads=[Blnf, B_const], writes=[Bbank[bP]])
                bS = bankA()
                S.op("pe", lambda e: e.matmul(psb(bS), U64, lnf[:], start=True, stop=True),
                     reads=[Blnf, B_const], writes=[Bbank[bS]])
                bD = bankA()

                def dec_fn(e):
                    inst = None
                    for hh in range(4):
                        inst = e.matmul(psb(bD)[:, 2 * hh:2 * hh + 2], lnf[:, hh * 128:(hh + 1) * 128],
                                        cm[:, 0, 63:128:64], start=True, stop=True)
                    return inst
                S.op("pe", dec_fn, reads=[Blnf, B_const], writes=[Bbank[bD]])
                S.op("act", lambda e: e.activation(out=eb[0][:], in_=psb(bP), func=AF.Exp, bias=math.log(0.5)),
                     reads=[Bbank[bP]], writes=[Beb[0]])
                S.op("dve", lambda e: e.tensor_tensor(out=qt[:], in0=q2[:], in1=eb[0][:], op=ALU.mult),
                     reads=[Bq2, Beb[0]], writes=[Bqt])
                S.op("act", lambda e: e.activation(out=eb[1][:], in_=psb(bP), func=AF.Exp, scale=-1.0),
                     reads=[Bbank[bP]], writes=[Beb[1]])
                S.op("dve", lambda e: e.tensor_tensor(out=kt[:], in0=k32[:], in1=eb[1][:], op=ALU.mult),
                     reads=[Bk32, Beb[1]], writes=[Bkt])
                S.op("act", lambda e: e.activation(out=eb[2][:], in_=psb(bS), func=AF.Exp),
                     reads=[Bbank[bS]], writes=[Beb[2]])
                S.op("dve", lambda e: e.tensor_tensor(out=kb[i2][:], in0=k32[:], in1=eb[2][:], op=ALU.mult),
                     reads=[Bk32, Beb[2]], writes=[Bkb[i2]])
                S.op("act", lambda e: e.activation(out=dec[i2][:], in_=psb(bD)[:, 0:8], func=AF.Exp),
                     reads=[Bbank[bD]], writes=[Bdec[i2]])

                bqk = bankA()
                pvq = psb16(bqk)

                def qk_tr(e):
                    inst = None
                    for hh in range(4):
                        inst = e.transpose(pvq[:, hh * 128:(hh + 1) * 128], qt[:, hh * 128:(hh + 1) * 128], ident[:])
                    for hh in range(4):
                        inst = e.transpose(pvq[:, (4 + hh) * 128:(5 + hh) * 128], kt[:, hh * 128:(hh + 1) * 128],
                                           ident[:])
                    return inst
                S.op("pe", qk_tr, reads=[Bqt, Bkt, B_const], writes=[Bbank[bqk]])
                S.op("dve", lambda e: e.tensor_copy(out=flat3(qkT[i2]), in_=pvq), reads=[Bbank[bqk]],
                     writes=[BqkT[i2]])

                R1 = rt[i2][:, 0:64]
                R2a = rt[i2][:, 64:96]
                R2b = rt[i2][:, 96:128]

                def dst_in(tb, nh):
                    if nh == 8:
                        return tb[:].rearrange("p (a j d) -> p a j d", a=2, j=4)
                    return tb[:, 0:128]

                def rope(bk_, nh, dst_ap, Bdst, t1, Bt1, t2, Bt2):
                    src = psb(bk_)[:, 0:nh * 64].rearrange("p (h d) -> p h d", d=64)
                    t1v = t1[:, 0:nh * 64].rearrange("p (h d) -> p h d", d=64)
                    t2v = t2[:, 0:nh * 64].rearrange("p (h d) -> p h d", d=64)
                    S.op("dve", lambda e: e.tensor_tensor(out=t1v, in0=src,
                                                          in1=R1.unsqueeze(1).to_broadcast([128, nh, 64]),
                                                          op=ALU.mult),
                         reads=[Bbank[bk_], Brt[i2]], writes=[Bt1])

                    def f2(e):
                        e.tensor_tensor(out=t2v[:, :, 0:32], in0=src[:, :, 32:64],
                                        in1=R2a.unsqueeze(1).to_broadcast([128, nh, 32]), op=ALU.mult)
                        return e.tensor_tensor(out=t2v[:, :, 32:64], in0=src[:, :, 0:32],
                                               in1=R2b.unsqueeze(1).to_broadcast([128, nh, 32]), op=ALU.mult)
                    S.op("dve", f2, reads=[Bbank[bk_], Brt[i2]], writes=[Bt2])
                    S.op("dve", lambda e: e.tensor_tensor(out=dst_ap, in0=dst_in(t1, nh), in1=dst_in(t2, nh),
                                                          op=ALU.add),
                         reads=[Bt1, Bt2], writes=[Bdst])

                bakv = proj(hTt, BhTt, 2560, 256, bankA())
                rope(bakv, 2, kr[:], Bkr, eb[0], Beb[0], eb[1], Beb[1])
                S.op("dve", lambda e: e.tensor_copy(
                    out=vaug[i3][:, :, 0:64], in_=psb(bakv)[:, 128:256].rearrange("p (g d) -> p g d", d=64)),
                     reads=[Bbank[bakv]], writes=[Bvaug[i3]])
                bkT = bankA()
                pvk = psb16(bkT)
                S.op("pe", lambda e: e.transpose(pvk[:, 0:128], kr[:], ident[:]),
                     reads=[Bkr, B_const], writes=[Bbank[bkT]])
                S.op("dve", lambda e: e.tensor_copy(out=kTb[i3][:], in_=pvk[:, 0:128]),
                     reads=[Bbank[bkT]], writes=[BkT[i3]])
                if t == 0:
                    S.op("dve", lambda e: e.tensor_copy(out=kTm[:], in_=kTb[0][:, PAD:128]), reads=[BkT[0]],
                         writes=[BkTm])
                    S.op("sp", lambda e: e.dma_start(out=vmeta[:], in_=vaug[0][PAD:128, :, :]), reads=[Bvaug[0]],
                         writes=[Bvmeta], dma="vmeta")
                    return
                baq = proj(hTt, BhTt, 2048, 512, bankA())
                rope(baq, 8, qr[:].rearrange("p (j a d) -> p a j d", a=2, d=64), Bqr, eb[2], Beb[2], eb[0], Beb[0])
                transposes(lambda i: qr[:, i * 128:(i + 1) * 128], 4, flat3(qT[i2]), BqT[i2], Bqr, "dve", bankA())

            def do_B1(t):
                i2, i3, p3 = t % 2, t % 3, (t - 1) % 3
                rr["b"] = 0
                bankB = bankB1
                og, Bog, ao, Bao = ogs[i2], Bogs[i2], aos[i2], Baos[i2]
                qk, Bqk = qkT[i2], BqkT[i2]
                vb, Bvb = v_bf[i2], Bv_bf[i2]
                kbt, Bkbt = kb[i2], Bkb[i2]
                dct, Bdct = dec[i2], Bdec[i2]
                bO = bankB()
                if t > 0:
                    bA = bankB()

                    def a_fn(e):
                        inst = None
                        for hh in range(4):
                            inst = e.matmul(psb(bA)[:, hh * 128:(hh + 1) * 128], qk[:, 4 + hh, :], qk[:, hh, :],
                                            start=True, stop=True)
                        return inst
                    S.op("pe", a_fn, reads=[Bqk], writes=[Bbank[bA]])
                    S.op("dve", lambda e: e.tensor_tensor(
                        out=A_sb[:], in0=psb(bA).rearrange("p (h t) -> p h t", h=4),
                        in1=mask64[:].unsqueeze(1).to_broadcast([128, 4, 128]), op=ALU.mult),
                         reads=[Bbank[bA], B_const], writes=[BA_sb])

                    def o1_fn(e):
                        inst = None
                        for hh in range(4):
                            inst = e.matmul(psb(bO)[:, hh * 128:(hh + 1) * 128], A_sb[:, hh, :],
                                            vb[:, hh * 128:(hh + 1) * 128], start=(hh == 0), stop=False,
                                            skip_group_check=True)
                        for hh in range(4):
                            inst = e.matmul(psb(bO)[0:64, hh * 128:(hh + 1) * 128], qk[:, hh, 0:64], Sbf[:, hh, :],
                                            start=False, stop=False, skip_group_check=True)
                        return inst
                    S.op("pe", o1_fn, reads=[BA_sb, Bvb, Bqk, BSbf], writes=[Bbank[bO]])
                bSa = bankB()
                bSb = bankB()

                def st_fn(e):
                    inst = None
                    for hh in range(4):
                        inst = e.matmul(psb(bSa)[:, hh * 128:(hh + 1) * 128], kbt[0:64, hh * 128:(hh + 1) * 128],
                                        vb[0:64, hh * 128:(hh + 1) * 128], start=True, stop=True)
                    for hh in range(4):
                        inst = e.matmul(psb(bSb)[:, hh * 128:(hh + 1) * 128], kbt[64:128, hh * 128:(hh + 1) * 128],
                                        vb[64:128, hh * 128:(hh + 1) * 128], start=True, stop=True)
                    return inst
                S.op("pe", st_fn, reads=[Bkbt, Bvb], writes=[Bbank[bSa], Bbank[bSb]])

                def upd(e, b, col):
                    inst = None
                    for hh in range(4):
                        inst = e.scalar_tensor_tensor(out=S32[:, hh, :], in0=S32[:, hh, :],
                                                      scalar=dct[:, 2 * hh + col:2 * hh + col + 1],
                                                      in1=psb(b)[:, hh * 128:(hh + 1) * 128],
                                                      op0=ALU.mult, op1=ALU.add)
                    return inst
                S.op("dve", lambda e: upd(e, bSa, 0), reads=[BS32, Bdct, Bbank[bSa], BSbf], writes=[BS32])
                S.op("act", lambda e: e.copy(out=flat3(Sbf), in_=flat3(S32)), reads=[BS32], writes=[BSbf])
                if t > 0:
                    def o2_fn(e):
                        inst = None
                        for hh in range(4):
                            inst = e.matmul(psb(bO)[64:128, hh * 128:(hh + 1) * 128], qk[:, hh, 64:128], Sbf[:, hh, :],
                                            start=False, stop=(hh == 3), skip_group_check=True)
                        return inst
                    S.op("pe", o2_fn, reads=[Bqk, BSbf, Bbank[bO]], writes=[Bbank[bO]])
                S.op("dve", lambda e: upd(e, bSb, 1), reads=[BS32, Bdct, Bbank[bSb], BSbf], writes=[BS32])
                S.op("act", lambda e: e.copy(out=flat3(Sbf), in_=flat3(S32)), reads=[BS32], writes=[BSbf])
                if t == 0:
                    return

                def sq_fn(e):
                    inst = None
                    for hh in range(4):
                        inst = e.activation(out=og[:, hh * 128:(hh + 1) * 128], in_=psb(bO)[:, hh * 128:(hh + 1) * 128],
                                            func=AF.Square, accum_out=ss[:, hh:hh + 1])
                    return inst
                S.op("act", sq_fn, reads=[Bbank[bO]], writes=[Bss, Bog])
                S.op("pool", lambda e: e.tensor_scalar(out=rsn[:], in0=ss[:], scalar1=1.0 / 128, scalar2=EPS,
                                                       op0=ALU.mult, op1=ALU.add), reads=[Bss], writes=[Brsn])
                S.op("pool", lambda e: e.tensor_tensor(out=rsn[:], in0=rsn[:], in1=cneg[:, 0:4], op=ALU.pow),
                     reads=[Brsn, B_const], writes=[Brsn])

                def og_fn(e):
                    inst = None
                    for hh in range(4):
                        inst = e.scalar_tensor_tensor(out=og[:, hh * 128:(hh + 1) * 128],
                                                      in0=psb(bO)[:, hh * 128:(hh + 1) * 128],
                                                      scalar=rsn[:, hh:hh + 1],
                                                      in1=gn2[i2][:, hh * 128:(hh + 1) * 128],
                                                      op0=ALU.mult, op1=ALU.mult)
                    return inst
                S.op("dve", og_fn, reads=[Bbank[bO], Brsn, Bgn2[i2], Bog], writes=[Bog])

                qT2 = flat3(qT[i2])

                def scores(kT_ap, nkeys, E, BE, Bk, mask):
                    bs0, bs1 = bankB(), bankB()

                    def f(e):
                        e.matmul(psb(bs0)[0:nkeys, :], kT_ap[0:64, 0:nkeys], qT2[0:64, :], start=True, stop=True)
                        return e.matmul(psb(bs1)[0:nkeys, :], kT_ap[64:128, 0:nkeys], qT2[64:128, :], start=True,
                                        stop=True)
                    S.op("pe", f, reads=[Bk, BqT[i2]], writes=[Bbank[bs0], Bbank[bs1]])
                    S.op("act", lambda e: e.activation(out=E[0:nkeys, 0, :], in_=psb(bs0)[0:nkeys, :], func=AF.Exp,
                                                       scale=0.125), reads=[Bbank[bs0]], writes=[BE])
                    S.op("act", lambda e: e.activation(out=E[0:nkeys, 1, :], in_=psb(bs1)[0:nkeys, :], func=AF.Exp,
                                                       scale=0.125), reads=[Bbank[bs1], BE], writes=[BE])
                    Ev = E[:].rearrange("p g (j q) -> p (g j) q", q=128)
                    if mask == "cur":
                        S.op("pool", lambda e: e.affine_select(
                            out=Ev, in_=Ev, pattern=[[0, 8], [1, 128]], compare_op=ALU.is_ge, fill=0.0, base=0,
                            channel_multiplier=-1), reads=[BE], writes=[BE])
                    elif mask == "prev":
                        S.op("pool", lambda e: e.affine_select(
                            out=Ev, in_=Ev, pattern=[[0, 8], [-1, 128]], compare_op=ALU.is_ge, fill=0.0, base=-1,
                            channel_multiplier=1), reads=[BE], writes=[BE])

                scores(kTb[i3], 128, Ecur, BEcur, BkT[i3], "cur")
                has_prev = t >= 2
                if has_prev:
                    scores(kTb[p3], 128, Eprev, BEprev, BkT[p3], "prev")
                scores(kTm, 16, Emeta, BEmeta, BkTm, None)
                bo = [bankB(), bankB()]

                def pv_fn(e):
                    inst = None
                    for h8 in range(8):
                        g, j = h8 // 4, h8 % 4
                        o_ap = psb(bo[g])[:, j * 65:(j + 1) * 65]
                        inst = e.matmul(o_ap, Ecur[:, g, j * 128:(j + 1) * 128], vaug[i3][:, g, :], start=True,
                                        stop=False)
                        if has_prev:
                            inst = e.matmul(o_ap, Eprev[:, g, j * 128:(j + 1) * 128], vaug[p3][:, g, :],
                                            start=False, stop=False)
                        inst = e.matmul(o_ap, Emeta[0:16, g, j * 128:(j + 1) * 128], vmeta[0:16, g, :], start=False,
                                        stop=True)
                    return inst
                rds = [BEcur, BEmeta, Bvaug[i3], Bvmeta] + ([BEprev, Bvaug[p3]] if has_prev else [])
                S.op("pe", pv_fn, reads=rds, writes=[Bbank[bo[0]], Bbank[bo[1]]])
                for g in range(2):
                    ov = psb(bo[g])[:, 0:260].rearrange("p (j d) -> p j d", d=65)
                    S.op("dve", (lambda e, ov=ov, g=g: e.tensor_tensor(
                        out=den[:, 4 * g:4 * g + 4].unsqueeze(2), in0=ov[:, :, 64:65],
                        in1=esink[:, 4 * g:4 * g + 4].unsqueeze(2), op=ALU.add)),
                         reads=[Bbank[bo[g]], B_const, Bden], writes=[Bden])
                S.op("dve", lambda e: e.reciprocal(out=rden[:], in_=den[:]), reads=[Bden], writes=[Brden])
                for g in range(2):
                    ov = psb(bo[g])[:, 0:260].rearrange("p (j d) -> p j d", d=65)
                    S.op("dve", (lambda e, ov=ov, g=g: e.tensor_tensor(
                        out=ao[:, 256 * g:256 * g + 256].rearrange("p (j d) -> p j d", d=64), in0=ov[:, :, 0:64],
                        in1=rden[:, 4 * g:4 * g + 4].unsqueeze(2).to_broadcast([128, 4, 64]), op=ALU.mult)),
                         reads=[Bbank[bo[g]], Brden, Bao], writes=[Bao])

            def do_B2(t):
                i2, i3 = t % 2, t % 3
                if t == 0:
                    return
                bankB = bankB2
                hTt, BhTt = hT[i3], BhT[i3]
                og, Bog, ao, Bao = ogs[i2], Bogs[i2], aos[i2], Baos[i2]
                for gi in range(4):
                    bgt = proj(hTt, BhTt, 2816 + gi * 512, 512, bankB())
                    S.op("act", (lambda e, b=bgt, gi=gi: e.activation(out=thG[:, gi * 512:(gi + 1) * 512],
                                                                      in_=psb(b), func=AF.Tanh, scale=0.5)),
                         reads=[Bbank[bgt]], writes=[BthG])

                def branch(src, Bsrc, w_sb, Bw):
                    transposes(lambda i: src[:, i * 128:(i + 1) * 128], 4, flat3(oT), BoT, Bsrc, "act", bankB())
                    pp = pair()

                    def f(e):
                        inst = None
                        for n in range(2):
                            for c in range(4):
                                inst = e.matmul(psb(pp + n), oT[:, c, :], w_sb[:, c, n * 512:(n + 1) * 512],
                                                start=(c == 0), stop=(c == 3))
                        return inst
                    S.op("pe", f, reads=[BoT, Bw], writes=[Bbank[pp], Bbank[pp + 1]])
                    return pp

                pp1 = branch(og, Bog, wbh_sb, B_wbh)
                for n in range(2):
                    S.op("dve", (lambda e, n=n: e.scalar_tensor_tensor(
                        out=m1[:, n * 512:(n + 1) * 512], in0=thG[:, n * 512:(n + 1) * 512], scalar=1.0,
                        in1=psb(pp1 + n), op0=ALU.add, op1=ALU.mult)),
                         reads=[BthG, Bbank[pp1 + n], Bm1], writes=[Bm1])
                pp2 = branch(ao, Bao, wba_sb, B_wba)
                for n in range(2):
                    S.op("dve", (lambda e, n=n: e.scalar_tensor_tensor(
                        out=r32[:, n * 512:(n + 1) * 512], in0=thG[:, 1024 + n * 512:1024 + (n + 1) * 512],
                        scalar=1.0, in1=psb(pp2 + n), op0=ALU.add, op1=ALU.mult)),
                         reads=[BthG, Bbank[pp2 + n], Br32], writes=[Br32])
                S.op("dve", lambda e: e.tensor_tensor(out=mm[:], in0=m1[:], in1=r32[:], op=ALU.add),
                     reads=[Bm1, Br32], writes=[Bmm])
                transposes(lambda i: mm[:, i * 128:(i + 1) * 128], 8, flat3(mT), BmT, Bmm, "act", bankB())
                pp3 = pair()

                def wo_fn(e):
                    inst = None
                    for n in range(2):
                        for c in range(8):
                            inst = e.matmul(psb(pp3 + n), mT[:, c, :], wo_sb[:, c, n * 512:(n + 1) * 512],
                                            start=(c == 0), stop=(c == 7))
                    return inst
                S.op("pe", wo_fn, reads=[BmT, B_wo], writes=[Bbank[pp3], Bbank[pp3 + 1]])
                for n in range(2):
                    S.op("dve", (lambda e, n=n: e.scalar_tensor_tensor(
                        out=r32[:, n * 512:(n + 1) * 512], in0=psb(pp3 + n), scalar=0.5 / ALPHA,
                        in1=hA[i3][:, n * 512:(n + 1) * 512], op0=ALU.mult, op1=ALU.add)),
                         reads=[Bbank[pp3 + n], BhA[i3], Br32], writes=[Br32])
                ln_stats(stat[1], r32, Br32, r32, Br32, G1, B1, EPS / (ALPHA * ALPHA))
                S.op("act", lambda e: e.copy(out=mm[:], in_=r32[:]), reads=[Br32], writes=[Bmm])
                S.op("sp", lambda e: e.dma_start(out=h1_d[t * 128:(t + 1) * 128, :], in_=r32[:]),
                     reads=[Br32], dma="h1o")
                transposes(lambda i: mm[:, i * 128:(i + 1) * 128], 8, flat3(h1T), Bh1T, Bmm, "act", bankB())
                S.op("sp", lambda e: e.dma_start(out=h1T_d[t], in_=flat3(h1T)), reads=[Bh1T], dma="h1To")

            def merge(la, lb):
                out_, ia, ib = [], 0, 0
                na, nb = len(la), len(lb)
                while ia < na or ib < nb:
                    if ib >= nb or (ia < na and ia * nb <= ib * na):
                        out_.append(la[ia]); ia += 1
                    else:
                        out_.append(lb[ib]); ib += 1
                return out_

            def merge3(ls):
                ls = [l for l in ls if l]
                if not ls:
                    return []
                out_ = ls[0]
                tot = len(ls[0])
                for l in ls[1:]:
                    out_ = merge(out_, l)
                return out_

            order = list(setup_ops)
            S.cur = lst = []
            load_tile(0)
            do_A(0)
            order += lst
            for t in range(NT + 1):
                S.cur = lB2 = []
                if t >= 1:
                    do_B2(t - 1)
                S.cur = lB1 = []
                if t < NT:
                    do_B1(t)
                S.cur = lA = []
                if t + 1 < NT:
                    do_A(t + 1)
                extra = []
                if t >= min(6, NT):
                    take = len(cv_ops) if t == NT else 3
                    extra, cv_ops[:] = cv_ops[:take], cv_ops[take:]
                order += merge3([lB2, lB1, lA, extra]) if PIPELINE else (lB2 + lB1 + lA + extra)
            S.ops = order

            if limit1 is not None:
                S.ops = S.ops[:limit1]
            S.finalize()
            S.emit(nc, block, sems, {k: dsem(k) for k in S.dma_counts})
            final = {e: 0 for e in Sched.ENGS}
            for o in S.ops:
                if o.dma is None and o.milestone:
                    final[o.eng] = max(final[o.eng], o.semval)
            dma_final = {k: 16 * v for k, v in S.dma_counts.items()}

            def sp_fence(e):
                for k, v in final.items():
                    if v > 0:
                        e.wait_ge(sems[k], v)
                for k, v in dma_final.items():
                    e.wait_ge(dsem(k), v)
                e.sem_inc(fence_sem, 1)
            block.sync(sp_fence)
            for en in ("tensor", "scalar", "vector", "gpsimd"):
                getattr(block, en)(lambda e: e.wait_ge(fence_sem, 1))

        S2 = Sched()
        p2 = ExitStack()
        with p2:
            def sb2(name, shape, dt):
                return p2.enter_context(nc.sbuf_tensor(name, list(shape), dt))

            NJ = D_FF // 128
            wfi_sb = sb2("wfi", [128, 8, 2 * D_FF], BF16)
            wfo_sb = sb2("wfo", [128, NJ, D], BF16)
            B_wfi = [[Buf("wfi%d_%d" % (b_, c)) for c in range(8)] for b_ in range(6)]
            B_wfo = [Buf("wfo%d" % j) for j in range(D_FF // 128)]
            G2 = sb2("G2", [128, D], F32); B2 = sb2("B2", [128, D], F32)
            cneg2 = sb2("cneg2", [128, 1], F32)
            Bc2 = Buf("c2")
            hTg = [sb2("hTg%d" % i, [128, 4, 8, 128], BF16) for i in range(2)]
            BhTg = [Buf("hTg%d" % i) for i in range(2)]
            h1in = [sb2("h1in%d" % i, [128, D], F32) for i in range(2)]
            Bh1in = [Buf("h1in%d" % i) for i in range(2)]
            gT = sb2("gT", [128, NJ, 512], BF16)
            BgT = [Buf("gT%d" % j) for j in range(NJ)]
            tha = [sb2("tha%d" % i, [128, 512], BF16) for i in range(2)]
            Btha = [Buf("tha%d" % i) for i in range(2)]
            s2b = [sb2("s2b%d" % i, [128, 512], F32) for i in range(2)]
            Bs2b = [Buf("s2b%d" % i) for i in range(2)]
            rr2 = [sb2("rr2_%d" % i, [128, D], F32) for i in range(2)]
            Brr2 = [Buf("rr2_%d" % i) for i in range(2)]
            st6b = sb2("st6b", [128, 2, 6], F32); Bst6b = Buf("st6b")
            mvb = sb2("mvb", [128, 2], F32); Bmvb = Buf("mvb")
            veb = sb2("veb", [128, 1], F32); Bveb = Buf("veb")
            rsb = sb2("rsb", [128, 1], F32); Brsb = Buf("rsb")
            Bbank2 = [Buf("bank2_%d" % i) for i in range(8)]

            def psb(i):
                return ps[:, i, :]

            wfi_v = wfi_bf.rearrange("(c p) n -> p c n", p=128)
            wfo_v = wfo_bf.rearrange("(j p) n -> p j n", p=128)
            slotB = [Buf("wslot%d" % i) for i in range(12)]
            kk = 0
            for blk in (0, 2, 3, 1, 4, 5):
                col0 = blk * 1024
                n = min(1024, 2 * D_FF - col0)
                for c in range(8):
                    S2.op("sp", (lambda e, c=c, col0=col0, n=n: e.dma_start(out=wfi_sb[:, c, col0:col0 + n],
                                                                            in_=wfi_v[:, c, col0:col0 + n])),
                          writes=[B_wfi[blk][c], slotB[kk % 12]], dma="wt%d" % (kk % 12))
                    kk += 1
            for j in range(NJ):
                S2.op("sp", (lambda e, j=j: e.dma_start(out=wfo_sb[:, j, :], in_=wfo_v[:, j, :])),
                      writes=[B_wfo[j], slotB[kk % 12]], dma="wt%d" % (kk % 12))
                kk += 1
            S2.op("sp", lambda e: e.dma_start(out=G2[:], in_=ln2_g.partition_broadcast(128)), writes=[Bc2], dma="g2")
            S2.op("sp", lambda e: e.dma_start(out=B2[:], in_=ln2_b.partition_broadcast(128)), writes=[Bc2], dma="b2")
            S2.op("dve", lambda e: e.memset(cneg2[:], -0.5), writes=[Bc2])

            rr2c = {"ab": 0}
            tiles2 = list(range(1, NT))
            groups = [tiles2[i:i + 4] for i in range(0, len(tiles2), 4)]

            def load_group(g):
                gi = g % 2
                for i, t in enumerate(groups[g]):
                    S2.op("sp", (lambda e, gi=gi, i=i, t=t: e.dma_start(
                        out=hTg[gi][:, i, :, :].rearrange("p c q -> p (c q)"), in_=h1T_d[t])),
                          writes=[BhTg[gi]], dma="hTg%d_%d" % (gi, i))

            def load_h1(n):
                fi = n % 2
                t = tiles2[n]
                S2.op("sp", lambda e: e.dma_start(out=h1in[fi][:], in_=h1_d[t * 128:(t + 1) * 128, :]),
                      writes=[Bh1in[fi]], dma="h1in%d" % fi)

            def ffn_in(g, j):
                gi = g % 2
                ntl = len(groups[g])
                ntok = ntl * 128
                ba = 2 * (rr2c["ab"] % 2)
                rr2c["ab"] += 1
                bu = ba + 1
                ai = j % 2

                def ff_in(e):
                    inst = None
                    for c in range(8):
                        inst = e.matmul(psb(ba)[:, 0:ntok], wfi_sb[:, c, j * 128:(j + 1) * 128],
                                        hTg[gi][:, 0:ntl, c, :], start=(c == 0), stop=(c == 7))
                    for c in range(8):
                        inst = e.matmul(psb(bu)[:, 0:ntok], wfi_sb[:, c, D_FF + j * 128:D_FF + (j + 1) * 128],
                                        hTg[gi][:, 0:ntl, c, :], start=(c == 0), stop=(c == 7))
                    return inst
                S2.op("pe", ff_in, reads=[BhTg[gi]] + B_wfi[j // 8] + B_wfi[(D_FF + j * 128) // 1024],
                      writes=[Bbank2[ba], Bbank2[bu]])
                S2.op("act", lambda e: e.activation(out=tha[ai][:, 0:ntok], in_=psb(ba)[:, 0:ntok], func=AF.Tanh,
                                                    scale=0.5),
                      reads=[Bbank2[ba]], writes=[Btha[ai]])
                S2.op("dve", lambda e: e.scalar_tensor_tensor(
                    out=s2b[ai][:, 0:ntok], in0=tha[ai][:, 0:ntok], scalar=1.0, in1=psb(ba)[:, 0:ntok],
                    op0=ALU.add, op1=ALU.mult),
                      reads=[Btha[ai], Bbank2[ba]], writes=[Bs2b[ai]])
                S2.op("dve", lambda e: e.tensor_tensor(
                    out=gT[:, j, 0:ntok], in0=s2b[ai][:, 0:ntok], in1=psb(bu)[:, 0:ntok], op=ALU.mult),
                      reads=[Bs2b[ai], Bbank2[bu]], writes=[BgT[j]])

            def ffn_out(n, i):
                t = tiles2[n]
                fi = n % 2
                pp = 4 + 2 * fi
                rb, Brb = rr2[fi], Brr2[fi]

                def ff_out(e):
                    inst = None
                    for nn in range(2):
                        for j in range(NJ):
                            inst = e.matmul(psb(pp + nn), gT[:, j, i * 128:(i + 1) * 128],
                                            wfo_sb[:, j, nn * 512:(nn + 1) * 512], start=(j == 0),
                                            stop=(j == NJ - 1))
                    return inst
                S2.op("pe", ff_out, reads=BgT + B_wfo, writes=[Bbank2[pp], Bbank2[pp + 1]])
                for nn in range(2):
                    S2.op("dve", (lambda e, nn=nn: e.scalar_tensor_tensor(
                        out=rb[:, nn * 512:(nn + 1) * 512], in0=psb(pp + nn), scalar=0.5 / ALPHA,
                        in1=h1in[fi][:, nn * 512:(nn + 1) * 512], op0=ALU.mult, op1=ALU.add)),
                          reads=[Bbank2[pp + nn], Bh1in[fi], Brb], writes=[Brb])

                def st_f(e):
                    e.bn_stats(out=st6b[:, 0, :], in_=rb[:, 0:512])
                    return e.bn_stats(out=st6b[:, 1, :], in_=rb[:, 512:1024])
                S2.op("dve", st_f, reads=[Brb], writes=[Bst6b])
                S2.op("dve", lambda e: e.bn_aggr(out=mvb[:], in_=st6b[:].rearrange("p a b -> p (a b)")),
                      reads=[Bst6b], writes=[Bmvb])
                S2.op("pool", lambda e: e.tensor_scalar(out=veb[:], in0=mvb[:, 1:2], scalar1=EPS / (ALPHA * ALPHA),
                                                        scalar2=None, op0=ALU.add), reads=[Bmvb], writes=[Bveb])
                S2.op("pool", lambda e: e.tensor_tensor(out=rsb[:], in0=veb[:], in1=cneg2[:, 0:1], op=ALU.pow),
                      reads=[Bveb, Bc2], writes=[Brsb])
                S2.op("dve", lambda e: e.scalar_tensor_tensor(out=rb[:], in0=rb[:], scalar=mvb[:, 0:1], in1=G2[:],
                                                              op0=ALU.subtract, op1=ALU.mult),
                      reads=[Brb, Bmvb, Bc2], writes=[Brb])
                S2.op("dve", lambda e: e.scalar_tensor_tensor(out=rb[:], in0=rb[:], scalar=rsb[:, 0:1], in1=B2[:],
                                                              op0=ALU.mult, op1=ALU.add),
                      reads=[Brb, Brsb, Bc2], writes=[Brb])
                S2.op("sp", lambda e: e.dma_start(out=out[(t - 1) * 128:t * 128, :], in_=rb[:]),
                      reads=[Brb], dma="out%d" % fi)

            if groups:
                load_group(0)
                load_h1(0)
            nflat = 0
            for g in range(len(groups)):
                if g + 1 < len(groups):
                    load_group(g + 1)
                for j in range(NJ):
                    ffn_in(g, j)
                for i in range(len(groups[g])):
                    if nflat + 1 < len(tiles2):
                        load_h1(nflat + 1)
                    ffn_out(nflat, i)
                    nflat += 1

            if skip2:
                S2.ops = []
            if limit2 is not None:
                S2.ops = S2.ops[:limit2]
            S2.finalize()
            S2.emit(nc, block, sems2, {k: dsem("p2_" + k) for k in S2.dma_counts})

            def sp_end(e):
                for k, v in S2.dma_counts.items():
                    e.wait_ge(dsem("p2_" + k), 16 * v)
            block.sync(sp_end)
    return nc


def _const_tables():
    half = 32
    inv = 10000.0 ** (-np.arange(half, dtype=np.float32) / half)
    pos = (np.arange(NTILES * 128, dtype=np.int32) - PAD).astype(np.float32)
    ang = pos[:, None] * inv[None, :]
    cos = np.cos(ang).astype(np.float32)
    sin = np.sin(ang).astype(np.float32)
    rope_t = np.concatenate([cos, cos, -sin, sin], axis=1).astype(np.float32)
    s = np.arange(128)
    same = (s[:, None] // 64) == (s[None, :] // 64)
    T64 = (same & (s[:, None] <= s[None, :])).astype(np.float32)
    U64 = (same & (s[:, None] > s[None, :])).astype(np.float32)
    ident = np.eye(128, dtype=np.float32)
    cmat = np.concatenate([T64, U64, ident], axis=1).astype(np.float32)
    return rope_t, cmat


def _in_maps(inputs, cores):
    rope_t, cmat = _const_tables()
    f = lambda a: np.ascontiguousarray(np.asarray(a, dtype=np.float32))
    shared = {
        "meta": f(inputs["meta_tokens"]),
        "ln_emb_g": f(inputs["ln_emb_g"]).reshape(1, D),
        "ln_emb_b": f(inputs["ln_emb_b"]).reshape(1, D),
        "w_in": f(inputs["w_in"])[0],
        "hg_lb": f(inputs["hg_lower_bounds"]).reshape(1, 1024),
        "hg_ng": f(inputs["hg_norm_g"]).reshape(1, 128),
        "sinks": f(inputs["attn_sinks"]).reshape(1, 8),
        "w_bhg": f(inputs["w_branch_hg"])[0],
        "w_batt": f(inputs["w_branch_attn"])[0],
        "w_out": f(inputs["w_out"])[0],
        "ln1_g": f(inputs["ln1_g"]).reshape(1, D),
        "ln1_b": f(inputs["ln1_b"]).reshape(1, D),
        "w_fi": f(inputs["w_ffn_in"])[0],
        "w_fo": f(inputs["w_ffn_out"])[0],
        "ln2_g": f(inputs["ln2_g"]).reshape(1, D),
        "ln2_b": f(inputs["ln2_b"]).reshape(1, D),
        "rope_t": rope_t,
        "cmat": cmat,
    }
    xs = np.asarray(inputs["x"], dtype=np.float32)
    return [dict(shared, x=np.ascontiguousarray(xs[b])) for b in cores]


def kernel(**inputs):
    nc = build(NTILES)
    in_maps = _in_maps(inputs, list(range(NCORES)))
    res = run_bass_kernel_spmd(nc, in_maps, core_ids=list(range(NCORES)))
    return np.stack([np.asarray(r["y_out"], dtype=np.float32) for r in res.results], axis=0)
```
